# Optimizing a Trainium2 kernel written in Bass

```python
import math
import jax
import jax.numpy as jnp
from jax import lax
import numpy as np

D_MODEL = 1024
BATCH = 4
SEQ = 4096
DEPTH = 1

CHUNK = 64
PLE_DIM = 256
D_FF = 2816
NORM_EPS = 1e-6

S5_WIDTH = 512
S5_GROUP = 16
S5_GROUPS = S5_WIDTH // S5_GROUP
S5_STATE = 64
S5_DT_MIN = 1e-3
S5_DT_MAX = 1e-1

HG_HEADS = 8
HG_EXPAND = 128
HG_WIDTH = HG_HEADS * HG_EXPAND

N_BRANCHES = 2
IN_SPLITS = (S5_WIDTH, HG_WIDTH, HG_WIDTH, HG_WIDTH, HG_WIDTH, D_MODEL, D_MODEL)
IN_COLS = S5_WIDTH + 4 * HG_WIDTH + N_BRANCHES * D_MODEL

kernel_name = 'hybrid_s5_hgrn2_macaron_block'


def rmsnorm(x, gain):
    xf = x.astype(jnp.float32)
    y = xf * lax.rsqrt(jnp.mean(jnp.square(xf), axis=-1, keepdims=True) + NORM_EPS)
    return (y * gain.astype(jnp.float32)).astype(x.dtype)


def swiglu(x, w_gate, w_up, w_down):
    return (jax.nn.silu(x @ w_gate) * (x @ w_up)) @ w_down


def split_columns(proj):
    pieces, start = [], 0
    for width in IN_SPLITS:
        pieces.append(proj[..., start:start + width])
        start += width
    return pieces


def s5_mixer(u, lam_re, lam_im, log_dt, b_re, b_im, c_re, c_im, d_skip):
    f32 = jnp.float32
    bsz, seqlen, _ = u.shape
    uf = u.astype(f32).reshape(bsz, seqlen, S5_GROUPS, S5_GROUP)
    lam = lax.complex(lam_re.astype(f32), lam_im.astype(f32))
    dt = jnp.exp(log_dt.astype(f32))[:, None]
    lam_bar = jnp.exp(lam * dt)
    b = lax.complex(b_re.astype(f32), b_im.astype(f32))
    b_bar = ((lam_bar - 1.0) / lam)[..., None] * b
    bu = jnp.einsum('blgh,gph->blgp', uf, b_bar)
    a = jnp.broadcast_to(lam_bar, (1, seqlen, S5_GROUPS, S5_STATE))

    def combine(left, right):
        a_l, b_l = left
        a_r, b_r = right
        return a_r * a_l, a_r * b_l + b_r

    _, states = lax.associative_scan(combine, (a, bu), axis=1)
    c = lax.complex(c_re.astype(f32), c_im.astype(f32))
    y = jnp.einsum('blgp,ghp->blgh', states, c).real
    y = y + d_skip.astype(f32).reshape(S5_GROUPS, S5_GROUP) * uf
    return y.reshape(bsz, seqlen, S5_WIDTH)


def hgrn2_mixer(q, f_logit, i_in, lower_bound):
    f32 = jnp.float32
    bsz, seqlen, _ = q.shape
    n_chunks = seqlen // CHUNK
    lb = lower_bound.astype(f32)
    z = f_logit.astype(f32)
    log_f = jnp.log(lb + (1.0 - lb) * jax.nn.sigmoid(z))
    k = (1.0 - lb) * jax.nn.sigmoid(-z)

    def to_chunks(t):
        return t.reshape(bsz, n_chunks, CHUNK, HG_HEADS, HG_EXPAND).transpose(1, 0, 3, 2, 4)

    qc = to_chunks(q.astype(f32) * (HG_EXPAND ** -0.5))
    kc = to_chunks(k)
    vc = to_chunks(i_in.astype(f32))
    gc = to_chunks(log_f)
    causal = jnp.tril(jnp.ones((CHUNK, CHUNK), dtype=bool))[:, :, None]

    def step(state, inp):
        q_c, k_c, v_c, g_c = inp
        gcum = jnp.cumsum(g_c, axis=2)
        o_inter = jnp.einsum('bhck,bhkv->bhcv', q_c * jnp.exp(gcum), state)
        diff = gcum[:, :, :, None, :] - gcum[:, :, None, :, :]
        decay = jnp.exp(jnp.where(causal, diff, -jnp.inf))
        scores = jnp.einsum('bhtk,bhsk,bhtsk->bhts', q_c, k_c, decay)
        o_intra = jnp.einsum('bhts,bhsv->bhtv', scores, v_c)
        g_last = gcum[:, :, -1:, :]
        k_dec = k_c * jnp.exp(g_last - gcum)
        new_state = (jnp.exp(g_last[:, :, 0, :])[..., None] * state
                     + jnp.einsum('bhsk,bhsv->bhkv', k_dec, v_c))
        return new_state, o_inter + o_intra

    s0 = jnp.zeros((bsz, HG_HEADS, HG_EXPAND, HG_EXPAND), f32)
    _, o = lax.scan(step, s0, (qc, kc, vc, gc))
    return o.transpose(1, 0, 3, 2, 4).reshape(bsz, seqlen, HG_WIDTH)


def head_rmsnorm(o, gain):
    bsz, seqlen, _ = o.shape
    oh = o.reshape(bsz, seqlen, HG_HEADS, HG_EXPAND)
    return rmsnorm(oh, gain.reshape(HG_HEADS, HG_EXPAND)).reshape(bsz, seqlen, HG_WIDTH)


def setup_inputs(seed: int = 0) -> dict:
    key = jax.random.key(seed)
    ks = jax.random.split(key, 40)
    f32 = jnp.float32

    def nrm(k, shape, scale):
        return jax.random.normal(k, shape, f32) * scale

    def gain(k, shape):
        return 1.0 + 0.01 * jax.random.normal(k, shape, f32)

    n_idx = jnp.arange(S5_STATE, dtype=f32)
    return {
        'x': nrm(ks[0], (BATCH, SEQ, D_MODEL), 1.0),
        'p': nrm(ks[1], (DEPTH, BATCH, SEQ, PLE_DIM), 1.0),
        'ffn1_norm': gain(ks[2], (DEPTH, D_MODEL)),
        'ffn1_w_gate': nrm(ks[3], (DEPTH, D_MODEL, D_FF), D_MODEL ** -0.5),
        'ffn1_w_up': nrm(ks[4], (DEPTH, D_MODEL, D_FF), D_MODEL ** -0.5),
        'ffn1_w_down': nrm(ks[5], (DEPTH, D_FF, D_MODEL), D_FF ** -0.5),
        'mix_norm': gain(ks[6], (DEPTH, D_MODEL)),
        'w_in': nrm(ks[7], (DEPTH, D_MODEL, IN_COLS), D_MODEL ** -0.5),
        's5_lam_re': -0.5 + 0.01 * jax.random.normal(ks[8], (DEPTH, S5_GROUPS, S5_STATE), f32),
        's5_lam_im': math.pi * n_idx + 0.01 * jax.random.normal(ks[9], (DEPTH, S5_GROUPS, S5_STATE), f32),
        's5_log_dt': jax.random.uniform(ks[10], (DEPTH, S5_GROUPS), f32,
                                        minval=math.log(S5_DT_MIN), maxval=math.log(S5_DT_MAX)),
        's5_b_re': nrm(ks[11], (DEPTH, S5_GROUPS, S5_STATE, S5_GROUP), (2.0 * S5_GROUP) ** -0.5),
        's5_b_im': nrm(ks[12], (DEPTH, S5_GROUPS, S5_STATE, S5_GROUP), (2.0 * S5_GROUP) ** -0.5),
        's5_c_re': nrm(ks[13], (DEPTH, S5_GROUPS, S5_GROUP, S5_STATE), 0.5),
        's5_c_im': nrm(ks[14], (DEPTH, S5_GROUPS, S5_GROUP, S5_STATE), 0.5),
        's5_d': nrm(ks[15], (DEPTH, S5_WIDTH), 1.0),
        's5_glu_val': nrm(ks[16], (DEPTH, S5_WIDTH, D_MODEL), S5_WIDTH ** -0.5),
        's5_glu_gate': nrm(ks[17], (DEPTH, S5_WIDTH, D_MODEL), S5_WIDTH ** -0.5),
        'hg_lower_bound': nrm(ks[18], (DEPTH + 1, HG_WIDTH), 0.1),
        'hg_out_norm': gain(ks[19], (DEPTH, HG_WIDTH)),
        'hg_w_out': nrm(ks[20], (DEPTH, HG_WIDTH, D_MODEL), HG_WIDTH ** -0.5),
        'w_merge_out': nrm(ks[21], (DEPTH, D_MODEL, D_MODEL), D_MODEL ** -0.5),
        'ffn2_norm': gain(ks[22], (DEPTH, D_MODEL)),
        'ffn2_w_gate': nrm(ks[23], (DEPTH, D_MODEL, D_FF), D_MODEL ** -0.5),
        'ffn2_w_up': nrm(ks[24], (DEPTH, D_MODEL, D_FF), D_MODEL ** -0.5),
        'ffn2_w_down': nrm(ks[25], (DEPTH, D_FF, D_MODEL), D_FF ** -0.5),
        'ple_norm': gain(ks[26], (DEPTH, D_MODEL)),
        'ple_w_gate': nrm(ks[27], (DEPTH, D_MODEL, D_MODEL), D_MODEL ** -0.5),
        'ple_w_proj': nrm(ks[28], (DEPTH, PLE_DIM, D_MODEL), PLE_DIM ** -0.5),
        'final_norm': gain(ks[29], (D_MODEL,)),
    }


def reference(x, p, ffn1_norm, ffn1_w_gate, ffn1_w_up, ffn1_w_down, mix_norm, w_in,
              s5_lam_re, s5_lam_im, s5_log_dt, s5_b_re, s5_b_im, s5_c_re, s5_c_im, s5_d,
              s5_glu_val, s5_glu_gate, hg_lower_bound, hg_out_norm, hg_w_out, w_merge_out,
              ffn2_norm, ffn2_w_gate, ffn2_w_up, ffn2_w_down, ple_norm, ple_w_gate,
              ple_w_proj, final_norm):
    lower_bounds = jnp.cumsum(jax.nn.softmax(hg_lower_bound.astype(jnp.float32), axis=0), axis=0)
    h = x
    for layer in range(DEPTH):
        h = h + 0.5 * swiglu(rmsnorm(h, ffn1_norm[layer]),
                             ffn1_w_gate[layer], ffn1_w_up[layer], ffn1_w_down[layer])

        u = rmsnorm(h, mix_norm[layer])
        s5_in, hg_q, hg_f, hg_i, hg_g, gate_a, gate_b = split_columns(u @ w_in[layer])

        y_s5 = s5_mixer(s5_in, s5_lam_re[layer], s5_lam_im[layer], s5_log_dt[layer],
                        s5_b_re[layer], s5_b_im[layer], s5_c_re[layer], s5_c_im[layer],
                        s5_d[layer])
        y_s5 = jax.nn.gelu(y_s5).astype(h.dtype)
        y_a = (y_s5 @ s5_glu_val[layer]) * jax.nn.sigmoid(y_s5 @ s5_glu_gate[layer])

        o = hgrn2_mixer(hg_q, hg_f, hg_i, lower_bounds[layer]).astype(h.dtype)
        o = head_rmsnorm(o, hg_out_norm[layer]) * jax.nn.silu(hg_g)
        y_b = o @ hg_w_out[layer]

        mixed = jax.nn.sigmoid(gate_a) * y_a + jax.nn.sigmoid(gate_b) * y_b
        h = h + mixed @ w_merge_out[layer]

        h = h + 0.5 * swiglu(rmsnorm(h, ffn2_norm[layer]),
                             ffn2_w_gate[layer], ffn2_w_up[layer], ffn2_w_down[layer])

        ple_gate = jax.nn.sigmoid(rmsnorm(h, ple_norm[layer]) @ ple_w_gate[layer])
        h = h + ple_gate * (p[layer] @ ple_w_proj[layer])
    return rmsnorm(h, final_norm)
```

```python
import os
import numpy as np
from contextlib import ExitStack
import concourse.bass as bass
import concourse.mybir as mybir
from concourse.bass_utils import run_bass_kernel_spmd

F32 = mybir.dt.float32
BF16 = mybir.dt.bfloat16
AF = mybir.ActivationFunctionType
ALU = mybir.AluOpType

NCORES = 8
T = 2048
D = 1024
DFF = 2816
SL = 512
NSL = T // SL
EPS = 1e-6
ND = 24


class K:
    def __init__(self, nc, es):
        self.nc = nc
        self.es = es
        self.eng = {'pe': nc.tensor, 'act': nc.scalar, 'dve': nc.vector, 'pool': nc.gpsimd, 'sp': nc.sync}
        self.sem = {e: es.enter_context(nc.semaphore("sem_" + e)) for e in self.eng}
        self.cnt = {e: 0 for e in self.eng}
        self.waited = {e: {} for e in self.eng}
        self.dsems = [es.enter_context(nc.semaphore("dsem%d" % i)) for i in range(ND)]
        self.dcnt = [0] * ND
        self.dnext = 0
        self.csem = es.enter_context(nc.semaphore("ccsem"))
        self.ccnt = 0
        self.res = {}

    def _semof(self, key):
        if key[0] == 'e':
            return self.sem[key[1]]
        if key[0] == 'd':
            return self.dsems[key[1]]
        return self.csem

    def _wait(self, e, key, val):
        if key == ('e', 'pe') and e == 'pe':
            return
        if self.waited[e].get(key, 0) >= val:
            return
        self.eng[e].wait_ge(self._semof(key), val)
        self.waited[e][key] = val

    def _deps(self, e, reads, writes):
        evs = {}

        def add(ev):
            if ev is None:
                return
            k, v = ev
            if evs.get(k, 0) < v:
                evs[k] = v
        for r in reads:
            st = self.res.get(r)
            if st:
                add(st['w'])
        for w in writes:
            st = self.res.get(w)
            if st:
                add(st['w'])
                for k, v in st['r'].items():
                    add((k, v))
        for k, v in evs.items():
            self._wait(e, k, v)

    def _commit(self, ev, reads, writes):
        k, v = ev
        for r in reads:
            st = self.res.setdefault(r, {'w': None, 'r': {}})
            if st['r'].get(k, 0) < v:
                st['r'][k] = v
        for w in writes:
            self.res[w] = {'w': ev, 'r': {}}

    def op(self, e, fn, reads=(), writes=(), sig=True):
        self._deps(e, reads, writes)
        inst = fn(self.eng[e])
        if not sig:
            self._commit((('e', e), self.cnt[e] + 1), reads, writes)
            return
        self.cnt[e] += 1
        inst.then_inc(self.sem[e], 1)
        self._commit((('e', e), self.cnt[e]), reads, writes)

    def dma(self, q, out, in_, reads=(), writes=(), **kw):
        i = self.dnext
        self.dnext = (i + 1) % ND
        if self.dcnt[i] > 0:
            self._wait(q, ('d', i), self.dcnt[i])
        self._deps(q, reads, writes)
        self.dcnt[i] += 16
        self.eng[q].dma_start(out=out, in_=in_, **kw).then_inc(self.dsems[i], 16)
        self._commit((('d', i), self.dcnt[i]), reads, writes)

    def collective(self, fn, reads=(), writes=()):
        self._deps('pool', reads, writes)
        self.ccnt += 1
        fn(self.eng['pool']).then_inc(self.csem)
        self._commit((('c', 0), self.ccnt), reads, writes)

    def finish(self, e, names):
        self._deps(e, names, ())


def _fence(k):
    for e in k.eng:
        for f in k.eng:
            if f != e and k.cnt[f] > 0:
                k._wait(e, ('e', f), k.cnt[f])
        for i in range(ND):
            if k.dcnt[i] > 0:
                k._wait(e, ('d', i), k.dcnt[i])
        if k.ccnt > 0:
            k._wait(e, ('c', 0), k.ccnt)


W_S5, W_Q, W_F, W_I, W_G, W_GA, W_GB = 0, 512, 1536, 2560, 3584, 4608, 5632
TWO_PI = 6.283185307179586
CW1 = 6.28125
CW2 = TWO_PI - CW1
GELU_K = 1.5957691216057308


def build(stages=('ffn1', 's5', 'hg', 'ffn2', 'ple'), final_norm=True, stop=None):
    nc = bass.Bass("TRN2", target_bir_lowering=False)

    def din(name, shape):
        return nc.dram_tensor(name, list(shape), F32, kind="ExternalInput").ap()

    xT = din("xT", [D, T])
    pT = din("pT", [256, T])
    gains = din("gains", [128, 6, 8])
    flag_d = din("flag", [128, 1])
    consts = din("consts", [128, 768])
    w1g = din("ffn1_w_gate", [D, DFF]); w1u = din("ffn1_w_up", [D, DFF]); w1d = din("ffn1_w_down", [DFF, D])
    w2g = din("ffn2_w_gate", [D, DFF]); w2u = din("ffn2_w_up", [D, DFF]); w2d = din("ffn2_w_down", [DFF, D])
    wpg = din("ple_w_gate", [D, D]); wpp = din("ple_w_proj", [256, D])
    w_in = din("w_in", [D, 6656])
    LT_d = din("s5_LT", [128, 3, 768]); BT_d = din("s5_BT", [128, 2, 768]); w5_d = din("s5_w5", [D, 768])
    LCc_d = din("s5_LCc", [128, 3, 16]); CC_d = din("s5_CC", [128, 2, 512]); D5_d = din("s5_D5", [128, 6])
    glv = din("s5_glu_val", [768, D]); glg = din("s5_glu_gate", [768, D])
    hglb_d = din("hg_lb", [128, 2, 8]); hggn_d = din("hg_gn", [128, 8])
    hgwo = din("hg_w_out", [D, D]); wmo = din("w_merge_out", [D, D])
    outT = nc.dram_tensor("outT", [D, T], F32, kind="ExternalOutput").ap()
    cin_s = nc.dram_tensor("cin_s", [128, 32], F32); cout_s = nc.dram_tensor("cout_s", [256, 32], F32)
    cin_h = [nc.dram_tensor("cin_h%d" % i, [128, 128], F32) for i in range(8)]
    cout_h = [nc.dram_tensor("cout_h%d" % i, [256, 128], F32) for i in range(8)]
    PAIRS = [[0, 1], [2, 3], [4, 5], [6, 7]]
    NBQ = [3, 3, 3, 3, 3, 1]

    with ExitStack() as es:
        _uid = [0]

        def alloc(st, name, shape, dt=F32):
            _uid[0] += 1
            return st.enter_context(nc.sbuf_tensor("%s_%d" % (name, _uid[0]), list(shape), dt))

        XN = alloc(es, "XN", [128, 8, T], BF16)
        G = alloc(es, "G", [128, 6, 8])
        ONES = alloc(es, "ONES", [128, 128], BF16)
        FLAG = alloc(es, "FLAG", [128, 1])
        PB = [es.enter_context(nc.psum_tensor("PB%d" % i, [128, 512], F32)) for i in range(8)]
        hst = [ExitStack()]
        sBh = []
        Hh = [alloc(hst[0], "H", [128, 8, T])]
        hsp = nc.dram_tensor("h_spill", [128, 8, T], F32)

        block = es.enter_context(nc.Block())

        @block.sync
        def _(sync):
            k = K(nc, es)
            sl = lambda s: slice(s * SL, (s + 1) * SL)

            def MM(out, lhsT, rhs, start, stop, reads, writes, sig=None):
                k.op('pe', lambda e: e.matmul(out, lhsT=lhsT, rhs=rhs, start=start, stop=stop), reads=reads, writes=writes,
                     sig=bool(stop) if sig is None else sig)
            pbr = lambda i: ('PB', i)

            for s_ in range(NSL):
                for kc in range(8):
                    k.dma('sp', Hh[0][:, kc, sl(s_)], xT[kc * 128:(kc + 1) * 128, sl(s_)], writes=[('H', kc, s_)])
            k.dma('sp', G[:], gains, writes=['G'])
            k.dma('sp', FLAG[:], flag_d, writes=['FLAG'])
            k.op('dve', lambda e: e.memset(ONES[:], 1.0), writes=['ONES'])

            def rmsnorm(st, emit, src=None, nparts=D, tag='n'):
                SQ = [alloc(st, "SQ%s%d" % (tag, i), [128, 512], BF16) for i in range(2)]
                RS = alloc(st, "RS" + tag, [128, 512])
                for s in range(NSL):
                    for kc in range(8):
                        q = SQ[kc % 2]
                        k.op('act', lambda e: e.activation(out=q[:], in_=Hh[0][:, kc, sl(s)], func=AF.Square),
                             reads=[('H', kc, s)], writes=[('SQ', kc % 2)])
                        MM(PB[6][:], lhsT=ONES[:], rhs=q[:], start=(kc == 0), stop=(kc == 7), reads=[('SQ', kc % 2), 'ONES'], writes=[pbr(6)], sig=True)
                    k.op('act', lambda e: e.activation(out=RS[:], in_=PB[6][:], func=AF.Sqrt, bias=EPS, scale=1.0 / D),
                         reads=[pbr(6)], writes=['RS'])
                    k.op('dve', lambda e: e.reciprocal(RS[:], RS[:]), reads=['RS'], writes=['RS'])
                    for kc in range(8):
                        emit(s, kc, RS)

            def norm_to_xn(st, which):
                def emit(s, kc, RS):
                    k.op('dve', lambda e: e.scalar_tensor_tensor(out=XN[:, kc, sl(s)], in0=Hh[0][:, kc, sl(s)],
                                                                 scalar=G[:, which, kc:kc + 1], in1=RS[:],
                                                                 op0=ALU.mult, op1=ALU.mult),
                         reads=[('H', kc, s), 'RS', 'G'], writes=[('XN', kc, s)])
                rmsnorm(st, emit)

            def ffn(which, wg, wu, wd):
                with ExitStack() as st:
                    norm_to_xn(st, which)
                    WG = [alloc(st, "WG%d" % i, [128, 8, 512], BF16) for i in range(2)]
                    WU = [alloc(st, "WU%d" % i, [128, 8, 512], BF16) for i in range(2)]
                    WD = [alloc(st, "WD%d" % i, [128, 4, D], BF16) for i in range(2)]
                    HM = [alloc(st, "HM%d" % i, [128, 512], BF16) for i in range(8)]
                    SG = [alloc(st, "SG%d" % i, [128, 512]) for i in range(2)]
                    pieces = [(0, 512), (512, 512), (1024, 512), (1536, 512), (2048, 512), (2560, 256)]
                    it = 0
                    for j, (f0, fw) in enumerate(pieces):
                        b = j % 2
                        nfc = fw // 128
                        k.dma('pool', WG[b][:, :, :fw], wg[:, f0:f0 + fw].rearrange("(kc p) f -> p kc f", p=128), writes=[('WG', b)])
                        k.dma('pool', WU[b][:, :, :fw], wu[:, f0:f0 + fw].rearrange("(kc p) f -> p kc f", p=128), writes=[('WU', b)])
                        k.dma('pool', WD[b][:, :nfc, :], wd[f0:f0 + fw, :].rearrange("(fc p) d -> p fc d", p=128), writes=[('WD', b)])
                        for s in range(NSL):
                            hs = (s % 2) * 4
                            for fc in range(nfc):
                                pb = it % 2
                                it += 1
                                for kc in range(8):
                                    MM(PB[pb][:], lhsT=WG[b][:, kc, fc * 128:(fc + 1) * 128],
                                                                  rhs=XN[:, kc, sl(s)], start=(kc == 0), stop=(kc == 7), reads=[('WG', b), ('XN', kc, s)], writes=[pbr(pb)])
                                for kc in range(8):
                                    MM(PB[2 + pb][:], lhsT=WU[b][:, kc, fc * 128:(fc + 1) * 128],
                                                                  rhs=XN[:, kc, sl(s)], start=(kc == 0), stop=(kc == 7), reads=[('WU', b), ('XN', kc, s)], writes=[pbr(2 + pb)])
                                k.op('act', lambda e: e.activation(out=SG[pb][:], in_=PB[pb][:], func=AF.Silu),
                                     reads=[pbr(pb)], writes=[('SG', pb)])
                                k.op('dve', lambda e: e.tensor_tensor(out=HM[hs + fc][:], in0=SG[pb][:], in1=PB[2 + pb][:], op=ALU.mult),
                                     reads=[('SG', pb), pbr(2 + pb)], writes=[('HM', hs + fc)])
                            for dc in range(8):
                                pb = 4 + dc % 2
                                for fc in range(nfc):
                                    MM(PB[pb][:], lhsT=WD[b][:, fc, dc * 128:(dc + 1) * 128],
                                                                  rhs=HM[hs + fc][:], start=(fc == 0), stop=(fc == nfc - 1), reads=[('WD', b), ('HM', hs + fc)], writes=[pbr(pb)])
                                k.op('dve', lambda e: e.scalar_tensor_tensor(out=Hh[0][:, dc, sl(s)], in0=PB[pb][:], scalar=0.5,
                                                                             in1=Hh[0][:, dc, sl(s)], op0=ALU.mult, op1=ALU.add),
                                     reads=[pbr(pb), ('H', dc, s)], writes=[('H', dc, s)])
                    _fence(k)

            def tail(st, nk, src_fn, src_res, wval_d, wgat_d, gate_col, after_slab=None):
                WV = alloc(st, "WV", [128, nk, D], BF16)
                WGT = alloc(st, "WGT", [128, nk, D], BF16) if wgat_d is not None else None
                WA = alloc(st, "WA", [128, 8, D], BF16)
                WM = alloc(st, "WM", [128, 8, D], BF16)
                MX = [alloc(st, "MX%d" % i, [128, 512], BF16) for i in range(16)]
                S1 = [alloc(st, "S1%d" % i, [128, 512]) for i in range(2)]
                S2 = [alloc(st, "S2%d" % i, [128, 512]) for i in range(2)]
                TT = [alloc(st, "TT%d" % i, [128, 512]) for i in range(2)]
                k.dma('pool', WV[:], wval_d.rearrange("(kc p) f -> p kc f", p=128), writes=['WV'])
                if WGT is not None:
                    k.dma('pool', WGT[:], wgat_d.rearrange("(kc p) f -> p kc f", p=128), writes=['WGT'])
                k.dma('pool', WA[:], w_in[:, gate_col:gate_col + D].rearrange("(kc p) f -> p kc f", p=128), writes=['WA'])
                k.dma('pool', WM[:], wmo.rearrange("(kc p) f -> p kc f", p=128), writes=['WM'])
                for s in range(NSL):
                    ms = (s % 2) * 8
                    for dc in range(8):
                        b = dc % 2
                        cs = slice(dc * 128, (dc + 1) * 128)
                        for kc in range(nk):
                            MM(PB[b][:], lhsT=WV[:, kc, cs], rhs=src_fn(kc, s),
                                                          start=(kc == 0), stop=(kc == nk - 1), reads=['WV', src_res(kc, s)], writes=[pbr(b)])
                        if WGT is not None:
                            for kc in range(nk):
                                MM(PB[2 + b][:], lhsT=WGT[:, kc, cs], rhs=src_fn(kc, s),
                                                              start=(kc == 0), stop=(kc == nk - 1), reads=['WGT', src_res(kc, s)], writes=[pbr(2 + b)])
                        for kc in range(8):
                            MM(PB[4 + b][:], lhsT=WA[:, kc, cs], rhs=XN[:, kc, sl(s)],
                                                          start=(kc == 0), stop=(kc == 7), reads=['WA', ('XN', kc, s)], writes=[pbr(4 + b)])
                        k.op('act', lambda e: e.activation(out=S2[b][:], in_=PB[4 + b][:], func=AF.Sigmoid),
                             reads=[pbr(4 + b)], writes=[('S2', b)])
                        if WGT is not None:
                            k.op('act', lambda e: e.activation(out=S1[b][:], in_=PB[2 + b][:], func=AF.Sigmoid),
                                 reads=[pbr(2 + b)], writes=[('S1', b)])
                            k.op('dve', lambda e: e.tensor_tensor(out=TT[b][:], in0=S1[b][:], in1=PB[b][:], op=ALU.mult),
                                 reads=[('S1', b), pbr(b)], writes=[('TT', b)])
                            k.op('pool', lambda e: e.tensor_tensor(out=MX[ms + dc][:], in0=TT[b][:], in1=S2[b][:], op=ALU.mult),
                                 reads=[('TT', b), ('S2', b)], writes=[('MX', ms + dc)])
                        else:
                            k.op('dve', lambda e: e.tensor_tensor(out=MX[ms + dc][:], in0=S2[b][:], in1=PB[b][:], op=ALU.mult),
                                 reads=[('S2', b), pbr(b)], writes=[('MX', ms + dc)])
                    for d2 in range(8):
                        b = 6 + d2 % 2
                        for dc in range(8):
                            MM(PB[b][:], lhsT=WM[:, dc, d2 * 128:(d2 + 1) * 128], rhs=MX[ms + dc][:],
                                                          start=(dc == 0), stop=(dc == 7), reads=['WM', ('MX', ms + dc)], writes=[pbr(b)])
                        k.op('dve', lambda e: e.tensor_tensor(out=Hh[0][:, d2, sl(s)], in0=PB[b][:], in1=Hh[0][:, d2, sl(s)], op=ALU.add),
                             reads=[pbr(b), ('H', d2, s)], writes=[('H', d2, s)])
                    if after_slab is not None:
                        after_slab(s)
                _fence(k)

            PE_ = 'dve'

            def tt(out, a, b, op, r, w):
                k.op(PE_, lambda e: e.tensor_tensor(out=out, in0=a, in1=b, op=op), reads=r, writes=w)

            def ts(out, a, s1, s2, op0, op1, r, w):
                if s2 is None:
                    k.op(PE_, lambda e: e.tensor_scalar(out=out, in0=a, scalar1=s1, scalar2=None, op0=op0), reads=r, writes=w)
                else:
                    k.op(PE_, lambda e: e.tensor_scalar(out=out, in0=a, scalar1=s1, scalar2=s2, op0=op0, op1=op1), reads=r, writes=w)

            def lam_l1(st, tag, LRE, LIM, LDT, W, rin):
                L1R = alloc(st, tag + "_l1r", [128, W]); L1I = alloc(st, tag + "_l1i", [128, W])
                sti = ExitStack()

                def t_(n, dt=F32):
                    return alloc(sti, "%s_%s" % (tag, n), [128, W], dt)
                DTt, A, TH, MAG, KF, HS, HC = [t_(n) for n in ("dt", "a", "th", "mag", "kf", "hs", "hc")]
                KI = t_("ki", mybir.dt.int32)
                n = lambda x: (tag, x)
                k.op('act', lambda e: e.activation(out=DTt[:], in_=LDT, func=AF.Exp), reads=rin, writes=[n('dt')])
                tt(A[:], LRE, DTt[:], ALU.mult, rin + [n('dt')], [n('a')])
                tt(TH[:], LIM, DTt[:], ALU.mult, rin + [n('dt')], [n('th')])
                k.op('act', lambda e: e.activation(out=MAG[:], in_=A[:], func=AF.Exp), reads=[n('a')], writes=[n('mag')])
                ts(KF[:], TH[:], 1.0 / TWO_PI, None, ALU.mult, None, [n('th')], [n('kf')])
                k.op(PE_, lambda e: e.tensor_copy(out=KI[:], in_=KF[:]), reads=[n('kf')], writes=[n('ki')])
                k.op(PE_, lambda e: e.tensor_copy(out=KF[:], in_=KI[:]), reads=[n('ki')], writes=[n('kf')])
                ts(A[:], KF[:], -CW1, None, ALU.mult, None, [n('kf'), n('mag')], [n('a')])
                tt(TH[:], TH[:], A[:], ALU.add, [n('th'), n('a')], [n('th')])
                ts(A[:], KF[:], -CW2, None, ALU.mult, None, [n('kf')], [n('a')])
                tt(TH[:], TH[:], A[:], ALU.add, [n('th'), n('a')], [n('th')])
                k.op('act', lambda e: e.activation(out=HS[:], in_=TH[:], func=AF.Sin, scale=0.5), reads=[n('th')], writes=[n('hs')])
                k.op('act', lambda e: e.activation(out=HC[:], in_=TH[:], func=AF.Sin, scale=-0.5, bias=float(np.pi / 2)),
                     reads=[n('th')], writes=[n('hc')])
                tt(A[:], HS[:], HC[:], ALU.mult, [n('hs'), n('hc')], [n('a')])
                ts(A[:], A[:], 2.0, None, ALU.mult, None, [n('a')], [n('a')])
                tt(L1I[:], A[:], MAG[:], ALU.mult, [n('a'), n('mag')], [n('l1i')])
                tt(A[:], HS[:], HS[:], ALU.mult, [n('hs')], [n('a')])
                ts(A[:], A[:], -2.0, 1.0, ALU.mult, ALU.add, [n('a')], [n('a')])
                tt(L1R[:], A[:], MAG[:], ALU.mult, [n('a'), n('mag')], [n('l1r')])
                _fence(k)
                sti.close()
                return L1R, L1I

            def cmul(outr, outi, ar, ai, br, bi, tmp, r, w, tag):
                t1, t2 = tmp
                tr = [(tag, 'ct1')]
                ti_ = [(tag, 'ct2')]
                tt(t1, ar, br, ALU.mult, r, tr)
                tt(t2, ai, bi, ALU.mult, r, ti_)
                tt(outr, t1, t2, ALU.subtract, tr + ti_, [w[0]])
                tt(t1, ar, bi, ALU.mult, r + [w[0]], tr)
                tt(t2, ai, br, ALU.mult, r + [w[0]], ti_)
                tt(outi, t1, t2, ALU.add, tr + ti_, [w[1]])

            def spill_slab(s):
                if 'hg' not in stages:
                    return
                for kc in range(8):
                    k.dma('sp', hsp.ap()[:, kc, sl(s)], Hh[0][:, kc, sl(s)], reads=[('H', kc, s)], writes=[('hsp', kc, s)])

            def s5_branch():
                with ExitStack() as sA:
                    U5 = alloc(sA, "U5", [128, 6, T], BF16)
                    with ExitStack() as s5:
                        X1B = alloc(s5, "X1B", [128, 16, 2, 258], BF16)
                        WIN = alloc(s5, "WIN", [128, 8, 2, 768], BF16)
                        LAM = alloc(s5, "LAM", [128, 9, 2, 16])
                        LR2 = alloc(s5, "LR2", [128, 16, 2]); NLI = alloc(s5, "NLI", [128, 16])
                        XINIT = alloc(s5, "XINIT", [128, 16, 2])
                        LAMB = alloc(s5, "LAMB", [128, 17, 2, 16]); NLB = alloc(s5, "NLB", [128, 17, 16]); L16 = alloc(s5, "L16", [128, 16, 2])
                        D5 = alloc(s5, "D5", [128, 6])
                        k.dma('sp', D5[:], D5_d, writes=['D5'])
                        for hf_ in range(2):
                          with ExitStack() as sp_:
                            wc = slice(hf_ * 384, (hf_ + 1) * 384)
                            LT = alloc(sp_, "LT", [128, 3, 384]); BT = alloc(sp_, "BT", [128, 2, 384])
                            k.dma('sp', LT[:], LT_d[:, :, wc], writes=['LT'])
                            k.dma('sp', BT[:], BT_d[:, :, wc], writes=['BT'])
                            L1R, L1I = lam_l1(sp_, "pt", LT[:, 0, :], LT[:, 1, :], LT[:, 2, :], 384, ['LT'])
                            nm = lambda x: ('pt', x)
                            tl = lambda n_: alloc(sp_, "pt_" + n_, [128, 384])
                            NR, D2, T1, T2, CR, CI, BBR, BBI = [tl(x) for x in ("nr", "d2", "t1", "t2", "cr", "ci", "bbr", "bbi")]
                            PR = [tl("pr0"), tl("pr1")]; PI = [tl("pi0"), tl("pi1")]
                            ts(NR[:], L1R[:], -1.0, None, ALU.add, None, [nm('l1r')], [nm('nr')])
                            tt(D2[:], LT[:, 0, :], LT[:, 0, :], ALU.mult, ['LT'], [nm('d2')])
                            tt(T1[:], LT[:, 1, :], LT[:, 1, :], ALU.mult, ['LT'], [nm('t1')])
                            tt(D2[:], D2[:], T1[:], ALU.add, [nm('d2'), nm('t1')], [nm('d2')])
                            k.op('dve', lambda e: e.reciprocal(D2[:], D2[:]), reads=[nm('d2')], writes=[nm('d2')])
                            tt(CR[:], NR[:], LT[:, 0, :], ALU.mult, [nm('nr'), 'LT'], [nm('cr')])
                            tt(T1[:], L1I[:], LT[:, 1, :], ALU.mult, [nm('l1i'), 'LT', nm('d2')], [nm('t1')])
                            tt(CR[:], CR[:], T1[:], ALU.add, [nm('cr'), nm('t1')], [nm('cr')])
                            tt(CR[:], CR[:], D2[:], ALU.mult, [nm('cr'), nm('d2')], [nm('cr')])
                            tt(CI[:], L1I[:], LT[:, 0, :], ALU.mult, [nm('l1i'), 'LT'], [nm('ci')])
                            tt(T1[:], NR[:], LT[:, 1, :], ALU.mult, [nm('nr'), 'LT', nm('cr')], [nm('t1')])
                            tt(CI[:], CI[:], T1[:], ALU.subtract, [nm('ci'), nm('t1')], [nm('ci')])
                            tt(CI[:], CI[:], D2[:], ALU.mult, [nm('ci'), nm('d2')], [nm('ci')])
                            cmul(BBR[:], BBI[:], CR[:], CI[:], BT[:, 0, :], BT[:, 1, :], (T1[:], T2[:]),
                                 [nm('cr'), nm('ci'), 'BT'], [nm('bbr'), nm('bbi')], 'pt')
                            k.op(PE_, lambda e: e.tensor_copy(out=WIN[:, 0, 0, wc], in_=BBR[:]), reads=[nm('bbr')], writes=[('WIN', 0, 0)])
                            k.op(PE_, lambda e: e.tensor_copy(out=WIN[:, 0, 1, wc], in_=BBI[:]), reads=[nm('bbi')], writes=[('WIN', 0, 1)])
                            cur_r, cur_i, rr = L1R[:], L1I[:], [nm('l1r'), nm('l1i')]
                            for j in range(1, 8):
                                if j > 1:
                                    pp = j % 2
                                    cmul(PR[pp][:], PI[pp][:], cur_r, cur_i, L1R[:], L1I[:], (T1[:], T2[:]),
                                         rr + [nm('l1r'), nm('l1i')], [nm(('pr', pp)), nm(('pi', pp))], 'pt')
                                    cur_r, cur_i, rr = PR[pp][:], PI[pp][:], [nm(('pr', pp)), nm(('pi', pp))]
                                cmul(WIN[:, j, 0, wc], WIN[:, j, 1, wc], cur_r, cur_i, BBR[:], BBI[:], (T1[:], T2[:]),
                                     rr + [nm('bbr'), nm('bbi')], [('WIN', j, 0), ('WIN', j, 1)], 'pt')
                            _fence(k)
                        if stop == 'prepT':
                            return
                        with ExitStack() as sc_:
                            LCc = alloc(sc_, "LCc", [128, 3, 16])
                            k.dma('sp', LCc[:], LCc_d, writes=['LCc'])
                            L1R, L1I = lam_l1(sc_, "pc", LCc[:, 0, :], LCc[:, 1, :], LCc[:, 2, :], 16, ['LCc'])
                            T1 = alloc(sc_, "pc_t1", [128, 16]); T2 = alloc(sc_, "pc_t2", [128, 16])
                            k.op(PE_, lambda e: e.tensor_copy(out=LAM[:, 1, 0, :], in_=L1R[:]), reads=[('pc', 'l1r')], writes=[('LAM', 1)])
                            k.op(PE_, lambda e: e.tensor_copy(out=LAM[:, 1, 1, :], in_=L1I[:]), reads=[('pc', 'l1i'), ('LAM', 1)], writes=[('LAM', 1)])
                            for j in range(2, 9):
                                cmul(LAM[:, j, 0, :], LAM[:, j, 1, :], LAM[:, j - 1, 0, :], LAM[:, j - 1, 1, :], LAM[:, 1, 0, :], LAM[:, 1, 1, :],
                                     (T1[:], T2[:]), [('LAM', j - 1), ('LAM', 1)], [('LAM', j), ('LAM', j)], 'pc')
                            k.op(PE_, lambda e: e.tensor_copy(out=LR2[:, :, 0], in_=LAM[:, 8, 0, :]), reads=[('LAM', 8)], writes=['LR2'])
                            k.op(PE_, lambda e: e.tensor_copy(out=LR2[:, :, 1], in_=LAM[:, 8, 0, :]), reads=[('LAM', 8), 'LR2'], writes=['LR2'])
                            ts(NLI[:], LAM[:, 8, 1, :], -1.0, None, ALU.mult, None, [('LAM', 8)], ['NLI'])
                            k.op(PE_, lambda e: e.tensor_copy(out=LAMB[:, 1, :, :], in_=LAM[:, 8, :, :]), reads=[('LAM', 8)], writes=['LAMB'])
                            for j in range(2, 17):
                                cmul(LAMB[:, j, 0, :], LAMB[:, j, 1, :], LAMB[:, j - 1, 0, :], LAMB[:, j - 1, 1, :], LAMB[:, 1, 0, :], LAMB[:, 1, 1, :],
                                     (T1[:], T2[:]), ['LAMB'], ['LAMB', 'LAMB'], 'pc')
                            ts(NLB[:], LAMB[:, :, 1, :], -1.0, None, ALU.mult, None, ['LAMB'], ['LAMB'])
                            k.op(PE_, lambda e: e.tensor_copy(out=L16[:, :, 0], in_=LAMB[:, 16, 0, :]), reads=['LAMB'], writes=['LAMB'])
                            k.op(PE_, lambda e: e.tensor_copy(out=L16[:, :, 1], in_=LAMB[:, 16, 0, :]), reads=['LAMB'], writes=['LAMB'])
                            _fence(k)
                        if stop == 'prepC':
                            return
                        with ExitStack() as sab:
                            with ExitStack() as sw5:
                                W5 = alloc(sw5, "W5", [128, 8, 768], BF16)
                                k.dma('pool', W5[:], w5_d.rearrange("(kc p) f -> p kc f", p=128), writes=['W5'])
                                for s in range(NSL):
                                    for q in range(6):
                                        b = 6 + (q % 2)
                                        for kc in range(8):
                                            MM(PB[b][:], lhsT=W5[:, kc, q * 128:(q + 1) * 128], rhs=XN[:, kc, sl(s)],
                                                                          start=(kc == 0), stop=(kc == 7), reads=['W5', ('XN', kc, s)], writes=[pbr(b)])
                                        k.op('act', lambda e: e.copy(out=U5[:, q, sl(s)], in_=PB[b][:]), reads=[pbr(b)], writes=[('U5', q, s)])
                                _fence(k)
                            if stop == 'proj':
                                return
                            E = alloc(sab, "E", [128, 16, 2, 256])
                            XS = [alloc(sab, "XS%d" % i, [128, 16, 2]) for i in range(2)]
                            AT = alloc(sab, "AT", [128, 16, 2]); BTt = alloc(sab, "BTt", [128, 16, 2])
                            RX = alloc(sab, "RX", [128, 32])
                            for s in range(NSL if stop != 'E1' else 1):
                                for q in range(6 if stop != 'E1' else 1):
                                    nb = NBQ[q]
                                    for part in range(2):
                                        for j in range(8):
                                            for b4 in range(nb):
                                                b = 3 * (q % 2) + b4
                                                c0 = part * 64
                                                MM(
                                                    PB[b][:, c0:c0 + 64],
                                                    lhsT=WIN[32 * b4:32 * b4 + 32, j, part, q * 128:(q + 1) * 128],
                                                    rhs=U5[32 * b4:32 * b4 + 32, q, sl(s)].rearrange("p (c s) -> p c s", s=8)[:, :, 7 - j],
                                                    start=(j == 0), stop=(j == 7), reads=[('WIN', j, part), ('U5', q, s)], writes=[pbr(b)])
                                    for b4 in range(nb):
                                        b = 3 * (q % 2) + b4
                                        for part in range(2):
                                            k.op('act' if part == 0 else 'dve',
                                                 (lambda e: e.copy(out=E[:, 3 * q + b4, part, 64 * s:64 * s + 64], in_=PB[b][:, part * 64:(part + 1) * 64])) if part == 0 else
                                                 (lambda e: e.tensor_copy(out=E[:, 3 * q + b4, part, 64 * s:64 * s + 64], in_=PB[b][:, part * 64:(part + 1) * 64])),
                                                 reads=[pbr(b)], writes=[('E', q, s, b4)])
                            er = [('E', q, s, b4) for q in range(6) for s in range(NSL) for b4 in range(3)]
                            if stop in ('E', 'E1'):
                                _fence(k)
                                return

                            E5 = E[:].rearrange("p a r (sb i) -> p a r sb i", i=16)
                            A4 = alloc(sab, "A4", [128, 16, 2, 16]); B4 = alloc(sab, "B4", [128, 16, 2, 16])
                            XSB = alloc(sab, "XSB", [128, 16, 2, 17])
                            bc3 = lambda a2: a2.unsqueeze(2).unsqueeze(3).broadcast_to([128, 16, 2, 16])
                            bc2 = lambda a2: a2.unsqueeze(2).broadcast_to([128, 16, 16])

                            def cstep(src, j, add_to, rr, wres):
                                k.op('dve', lambda e: e.tensor_tensor(out=A4[:], in0=src, in1=bc3(LAMB[:, j, 0, :]), op=ALU.mult),
                                     reads=rr + ['LAMB'], writes=['A4'])
                                k.op('dve', lambda e: e.tensor_tensor(out=B4[:, :, 0, :], in0=src[:, :, 1, :], in1=bc2(NLB[:, j, :]), op=ALU.mult),
                                     reads=rr + ['LAMB'], writes=['B40'])
                                k.op('dve', lambda e: e.tensor_tensor(out=B4[:, :, 1, :], in0=src[:, :, 0, :], in1=bc2(LAMB[:, j, 1, :]), op=ALU.mult),
                                     reads=rr + ['LAMB'], writes=['B41'])
                                k.op('dve', lambda e: e.tensor_tensor(out=A4[:], in0=A4[:], in1=B4[:], op=ALU.add),
                                     reads=['A4', 'B40', 'B41'], writes=['A4'])
                                k.op('dve', lambda e: e.tensor_tensor(out=add_to, in0=A4[:], in1=add_to, op=ALU.add),
                                     reads=['A4'] + rr, writes=wres)

                            for i in range(1, 16):
                                cstep(E5[:, :, :, :, i - 1], 1, E5[:, :, :, :, i], er if i == 1 else ['E5'], ['E5'])

                            def level1b(init_ap, init_res):
                                k.op('dve', lambda e: e.tensor_copy(out=XSB[:, :, :, 0], in_=init_ap), reads=init_res + ['XSB'], writes=['XSB'])
                                for sb_ in range(16):
                                    k.op('dve', lambda e: e.tensor_tensor(out=AT[:], in0=XSB[:, :, :, sb_], in1=L16[:], op=ALU.mult),
                                         reads=['XSB', 'LAMB'], writes=['AT'])
                                    k.op('dve', lambda e: e.tensor_tensor(out=BTt[:, :, 0], in0=XSB[:, :, 1, sb_], in1=NLB[:, 16, :], op=ALU.mult),
                                         reads=['XSB', 'LAMB'], writes=['BT0'])
                                    k.op('dve', lambda e: e.tensor_tensor(out=BTt[:, :, 1], in0=XSB[:, :, 0, sb_], in1=LAMB[:, 16, 1, :], op=ALU.mult),
                                         reads=['XSB', 'LAMB'], writes=['BT1'])
                                    k.op('dve', lambda e: e.tensor_tensor(out=AT[:], in0=AT[:], in1=BTt[:], op=ALU.add),
                                         reads=['AT', 'BT0', 'BT1'], writes=['AT'])
                                    k.op('dve', lambda e: e.tensor_tensor(out=XSB[:, :, :, sb_ + 1], in0=AT[:], in1=E5[:, :, :, sb_, 15], op=ALU.add),
                                         reads=['AT', 'E5'], writes=['XSB'])

                            k.op('pool', lambda e: e.memset(XINIT[:], 0.0), writes=['XINIT'])
                            level1b(XINIT[:], ['XINIT'])
                            k.op('pool', lambda e: e.tensor_copy(out=XS[0][:], in_=XSB[:, :, :, 16]), reads=['XSB'], writes=[('XS', 0)])
                            if stop == 'passA':
                                _fence(k)
                                return
                            k.dma('sp', cin_s.ap(), XS[0][:].rearrange("p a b -> p (a b)"), reads=[('XS', 0)], writes=['cin_s'])
                            k.collective(lambda e: e.collective_compute("AllGather", ALU.bypass, replica_groups=PAIRS,
                                                                        ins=[cin_s.ap().opt()], outs=[cout_s.ap().opt()]),
                                         reads=['cin_s'], writes=['cout_s'])
                            k.dma('sp', RX[:], cout_s.ap()[0:128, :], reads=['cout_s'], writes=['RX'])
                            k.op('pool', lambda e: e.tensor_scalar(out=XINIT[:].rearrange("p a b -> p (a b)"), in0=RX[:], scalar1=FLAG[:, 0:1],
                                                                   scalar2=None, op0=ALU.mult),
                                 reads=['RX', 'FLAG', 'XINIT'], writes=['XINIT'])
                            if stop == 'coll':
                                _fence(k)
                                return
                            level1b(XINIT[:], ['XINIT'])
                            for i in range(16):
                                cstep(XSB[:, :, :, 0:16], i + 1, E5[:, :, :, :, i], ['XSB', 'E5'], ['E5'])
                            k.op('pool', lambda e: e.tensor_copy(out=AT[:], in_=AT[:]), reads=['E5', 'AT'], writes=[('Ec', 255), 'AT'])
                            k.op('pool', lambda e: e.tensor_copy(out=X1B[:, :, :, 0], in_=XINIT[:]), reads=['XINIT'], writes=['X1B0'])
                            for P_ in range(16):
                                k.op('dve', lambda e: e.tensor_copy(out=X1B[:, P_, :, 1:257], in_=E[:, P_, :, :]), reads=[('Ec', 255)], writes=['X1B'])
                            _fence(k)
                        if stop == 'passB':
                            return
                        with ExitStack() as ssl:
                            CCOR = alloc(ssl, "CCOR", [128, 8, 2, 512], BF16)
                            C0 = alloc(ssl, "C0", [128, 2, 512], BF16)
                            with ExitStack() as spc:
                                CC = alloc(spc, "CC", [128, 2, 512])
                                k.dma('sp', CC[:], CC_d, writes=['CC'])
                                Wt = [alloc(spc, "cw%d" % i, [128, 512]) for i in range(4)]
                                NLP = alloc(spc, "NLP", [128, 16])
                                v3 = lambda ap: ap.rearrange("p (a b) -> p a b", a=16)
                                k.op(PE_, lambda e: e.tensor_copy(out=C0[:, 0, :], in_=CC[:, 0, :]), reads=['CC'], writes=[('C0', 0)])
                                ts(C0[:, 1, :], CC[:, 1, :], -1.0, None, ALU.mult, None, ['CC'], [('C0', 1)])
                                for s8 in range(8):
                                    lr = LAM[:, s8 + 1, 0, :].unsqueeze(2).broadcast_to([128, 16, 32])
                                    li = LAM[:, s8 + 1, 1, :].unsqueeze(2).broadcast_to([128, 16, 32])
                                    ts(NLP[:], LAM[:, s8 + 1, 1, :], -1.0, None, ALU.mult, None, [('LAM', s8 + 1)], ['NLP'])
                                    nli = NLP[:].unsqueeze(2).broadcast_to([128, 16, 32])
                                    tt(v3(Wt[0][:]), v3(CC[:, 0, :]), lr, ALU.mult, ['CC', ('LAM', s8 + 1)], ['cw0'])
                                    tt(v3(Wt[1][:]), v3(CC[:, 1, :]), li, ALU.mult, ['CC', ('LAM', s8 + 1)], ['cw1'])
                                    tt(CCOR[:, s8, 0, :], Wt[0][:], Wt[1][:], ALU.subtract, ['cw0', 'cw1'], [('CCOR', s8, 0)])
                                    tt(v3(Wt[2][:]), v3(CC[:, 0, :]), nli, ALU.mult, ['CC', 'NLP'], ['cw2'])
                                    tt(v3(Wt[3][:]), v3(CC[:, 1, :]), lr, ALU.mult, ['CC', ('LAM', s8 + 1)], ['cw3'])
                                    tt(CCOR[:, s8, 1, :], Wt[2][:], Wt[3][:], ALU.subtract, ['cw2', 'cw3'], [('CCOR', s8, 1)])
                                _fence(k)
                            ZB = [alloc(ssl, "ZB%d" % i, [128, 512], BF16) for i in range(12)]
                            YF = [alloc(ssl, "YF%d" % i, [128, 512]) for i in range(2)]
                            GT = [alloc(ssl, "GT%d" % i, [128, 512]) for i in range(2)]
                            v8 = lambda ap: ap.rearrange("p (c s) -> p c s", s=8)

                            def phase1(s, q):
                                nb = NBQ[q]
                                zo = 6 * (q % 2)
                                for j in range(8):
                                    for b4 in range(nb):
                                        band = slice(32 * b4, 32 * b4 + 32)
                                        for part in range(2):
                                            zb = (0, 2, 6)[b4] + part
                                            MM(
                                                v8(PB[zb][:])[:, :, j:8],
                                                lhsT=WIN[band, j, part, q * 128:(q + 1) * 128],
                                                rhs=v8(U5[band, q, sl(s)])[:, :, 0:8 - j],
                                                start=(j == 0), stop=(j == 7), reads=[('WIN', j, part), ('U5', q, s)], writes=[pbr(zb)])
                                for b4 in range(nb):
                                    for part in range(2):
                                        zb = (0, 2, 6)[b4] + part
                                        zt = zo + 2 * b4 + part
                                        if part == 0:
                                            k.op('act', lambda e: e.copy(out=ZB[zt][:], in_=PB[zb][:]), reads=[pbr(zb)], writes=[('ZB', zt)])
                                        else:
                                            k.op('dve', lambda e: e.tensor_copy(out=ZB[zt][:], in_=PB[zb][:]), reads=[pbr(zb)], writes=[('ZB', zt)])

                            def phase2(s, q):
                                yb = 4 + q % 2
                                nb = NBQ[q]
                                zo = 6 * (q % 2)
                                pr_ = slice(0, 32 * nb)
                                for b4 in range(nb):
                                    P = 3 * q + b4
                                    band = slice(32 * b4, 32 * b4 + 32)
                                    for part in range(2):
                                        zt = zo + 2 * b4 + part
                                        MM(PB[yb][band, :], lhsT=C0[:, part, 32 * P:32 * P + 32], rhs=ZB[zt][:],
                                                                      start=(part == 0), stop=False, reads=[('C0', part), ('ZB', zt)], writes=[pbr(yb)])
                                    for s8 in range(8):
                                        for part in range(2):
                                            MM(v8(PB[yb][band, :])[:, :, s8],
                                                                          lhsT=CCOR[:, s8, part, 32 * P:32 * P + 32],
                                                                          rhs=X1B[:, P, part, 64 * s:64 * s + 64],
                                                                          start=False, stop=(s8 == 7 and part == 1), reads=[('CCOR', s8, part), 'X1B', 'X1B0'], writes=[pbr(yb)])
                                f = q % 2
                                k.op('dve', lambda e: e.scalar_tensor_tensor(out=YF[f][pr_, :], in0=U5[pr_, q, sl(s)], scalar=D5[pr_, q:q + 1],
                                                                             in1=PB[yb][pr_, :], op0=ALU.mult, op1=ALU.add),
                                     reads=[('U5', q, s), 'D5', pbr(yb)], writes=[('YF', f)])
                                k.op('dve', lambda e: e.tensor_tensor(out=GT[f][pr_, :], in0=YF[f][pr_, :], in1=YF[f][pr_, :], op=ALU.mult),
                                     reads=[('YF', f)], writes=[('GT', f)])
                                k.op('dve', lambda e: e.tensor_scalar(out=GT[f][pr_, :], in0=GT[f][pr_, :], scalar1=0.044715, scalar2=1.0,
                                                                       op0=ALU.mult, op1=ALU.add),
                                     reads=[('GT', f)], writes=[('GT', f)])
                                k.op('dve', lambda e: e.tensor_tensor(out=GT[f][pr_, :], in0=GT[f][pr_, :], in1=YF[f][pr_, :], op=ALU.mult),
                                     reads=[('GT', f), ('YF', f)], writes=[('GT', f)])
                                k.op('act', lambda e: e.activation(out=GT[f][pr_, :], in_=GT[f][pr_, :], func=AF.Sigmoid, scale=GELU_K),
                                     reads=[('GT', f)], writes=[('GT', f)])
                                k.op('dve', lambda e: e.tensor_tensor(out=U5[pr_, q, sl(s)], in0=GT[f][pr_, :], in1=YF[f][pr_, :], op=ALU.mult),
                                     reads=[('GT', f), ('YF', f), ('U5', q, s)], writes=[('U5', q, s)])

                            items = [(s, q) for s in range(NSL) for q in range(6)]
                            for idx, (s, q) in enumerate(items):
                                phase1(s, q)
                                if idx > 0:
                                    phase2(*items[idx - 1])
                            phase2(*items[-1])
                            _fence(k)
                    if stop == 'slab':
                        return
                    with ExitStack() as stl:
                        tail(stl, 6, lambda kc, s: U5[:, kc, sl(s)], lambda kc, s: ('U5', kc, s), glv, glg, W_GA, after_slab=spill_slab)

            def hg_core(OG):
                with ExitStack() as sh:
                        CONST = alloc(sh, "CONST", [128, 768])
                        MASKC = CONST[:, 512:640]
                        IDb = alloc(sh, "IDb", [128, 128], BF16)
                        LBR = alloc(sh, "LBR", [128, 2, 8]); LB = alloc(sh, "LB", [128, 8]); OML = alloc(sh, "OML", [128, 8])
                        NOML = alloc(sh, "NOML", [128, 8]); GN = alloc(sh, "GN", [128, 8])
                        k.dma('sp', CONST[:], consts, writes=['CONST'])
                        k.dma('sp', LBR[:], hglb_d, writes=['LBR'])
                        k.dma('sp', GN[:], hggn_d, writes=['GN'])
                        k.op('dve', lambda e: e.tensor_copy(out=IDb[:], in_=CONST[:, 640:768]), reads=['CONST'], writes=['IDb'])
                        k.op('dve', lambda e: e.tensor_tensor(out=LB[:], in0=LBR[:, 0, :], in1=LBR[:, 1, :], op=ALU.subtract),
                             reads=['LBR'], writes=['LB'])
                        k.op('act', lambda e: e.activation(out=LB[:], in_=LB[:], func=AF.Sigmoid), reads=['LB'], writes=['LB'])
                        k.op('dve', lambda e: e.tensor_scalar(out=OML[:], in0=LB[:], scalar1=-1.0, scalar2=1.0, op0=ALU.mult, op1=ALU.add),
                             reads=['LB'], writes=['OML'])
                        k.op('dve', lambda e: e.tensor_scalar(out=NOML[:], in0=LB[:], scalar1=-1.0, scalar2=None, op0=ALU.add),
                             reads=['LB'], writes=['NOML'])
                        NB_ = 2
                        WQ2, WF2, WI2, WGt2 = [[alloc(sh, n_ + str(i), [128, 8, 128], BF16) for i in range(2)] for n_ in ("WQ", "WF", "WI", "WGt")]
                        QTs = [alloc(sh, "QT%d" % i, [128, T], BF16) for i in range(NB_)]
                        KTs = [alloc(sh, "KT%d" % i, [128, T], BF16) for i in range(NB_)]
                        KTTs = [alloc(sh, "KTT%d" % i, [128, 16, 128], BF16) for i in range(NB_)]
                        VTs = [alloc(sh, "VT%d" % i, [128, 16, 128], BF16) for i in range(NB_)]
                        SGts = [alloc(sh, "SGt%d" % i, [128, T], BF16) for i in range(NB_)]
                        DKs = [alloc(sh, "DK%d" % i, [128, 32]) for i in range(NB_)]
                        SINITs = [alloc(sh, "SINIT%d" % i, [128, 128]) for i in range(NB_)]
                        SFINs = [alloc(sh, "SFIN%d" % i, [128, 128]) for i in range(NB_)]
                        RXHs = [alloc(sh, "RXH%d" % i, [128, 128]) for i in range(NB_)]
                        SGM2, LF2, GC2, EG2, ENG2, KK2, SGG2 = [[alloc(sh, n_ + str(i), [128, 512]) for i in range(2)]
                                                                for n_ in ("SGM", "LF", "GC", "EG", "ENG", "KK", "SGG")]
                        TPa = [alloc(sh, "TPa%d" % i, [128, 128]) for i in range(2)]
                        TPb = [alloc(sh, "TPb%d" % i, [128, 128]) for i in range(2)]
                        SALL = alloc(sh, "SALL", [128, 32, 128], BF16)
                        SC = [alloc(sh, "SC%d" % i, [128, 128], BF16) for i in range(2)]
                        OSQ = alloc(sh, "OSQ", [128, 512], BF16); ORS = alloc(sh, "ORS", [128, 512]); OT = alloc(sh, "OT", [128, 512])
                        PT7 = PB[7][:].bitcast(BF16)
                        SCALE = float(128 ** -0.5)

                        def load_w(h_):
                            wb_ = h_ % 2
                            for (Wl, c0, nm_) in ((WQ2, W_Q, 'WQ'), (WF2, W_F, 'WF'), (WI2, W_I, 'WI'), (WGt2, W_G, 'WGt')):
                                k.dma('pool', Wl[wb_][:], w_in[:, c0 + h_ * 128:c0 + (h_ + 1) * 128].rearrange("(kc p) f -> p kc f", p=128),
                                      writes=[(nm_, wb_)])

                        def kv(hb, c, which):
                            t_i, hh = c // 2, c % 2
                            pb = 4 + hh
                            co = 0
                            MM(PB[pb][:, co:co + 128], lhsT=KTTs[hb][64 * hh:64 * hh + 64, t_i, :],
                                                          rhs=VTs[hb][64 * hh:64 * hh + 64, t_i, :], start=True, stop=True, reads=[('KTT', hb, t_i // 4), ('VT', hb, t_i // 4)], writes=[('PBh', pb, which)])
                            return pb

                        def chain(hb, init_ap, init_res, store):
                            DK = DKs[hb]
                            TP = TPb if store else TPa
                            tn = 'TPb' if store else 'TPa'
                            for c in range(32):
                                if c > 0:
                                    yield
                                wh = 0
                                pb = kv(hb, c, wh)
                                dst, src = TP[c % 2], TP[(c + 1) % 2]
                                if c == 0:
                                    if init_ap is None:
                                        k.op('dve', lambda e: e.tensor_copy(out=dst[:], in_=PB[pb][:, wh * 128:wh * 128 + 128]), reads=[('PBh', pb, wh)], writes=[(tn, 0)])
                                    else:
                                        k.op('dve', lambda e: e.tensor_tensor(out=dst[:], in0=init_ap, in1=PB[pb][:, wh * 128:wh * 128 + 128], op=ALU.add),
                                             reads=init_res + [('PBh', pb, wh)], writes=[(tn, 0)])
                                        k.op('act', lambda e: e.copy(out=SALL[:, 0, :], in_=init_ap), reads=init_res, writes=[('SALL', 0)])
                                else:
                                    if store:
                                        k.op('act', lambda e: e.activation(out=SALL[:, c, :], in_=src[:], func=AF.Copy, scale=DK[:, c - 1:c]),
                                             reads=[(tn, (c + 1) % 2), ('DK', hb, (c - 1) // 8)], writes=[('SALL', c)])
                                    k.op('dve', lambda e: e.scalar_tensor_tensor(out=dst[:], in0=src[:], scalar=DK[:, c - 1:c], in1=PB[pb][:, wh * 128:wh * 128 + 128],
                                                                                 op0=ALU.mult, op1=ALU.add),
                                         reads=[(tn, (c + 1) % 2), ('DK', hb, (c - 1) // 8), ('PBh', pb, wh)], writes=[(tn, c % 2)])

                        def slab_a1(hd, s):
                            hb = hd % NB_
                            wb = hd % 2
                            WQ, WF, WI, WGt = WQ2[wb], WF2[wb], WI2[wb], WGt2[wb]
                            QT, KT, VT, SGt, DK = QTs[hb], KTs[hb], VTs[hb], SGts[hb], DKs[hb]
                            tb = s % 2
                            SGM, LF, KK, GC, EG, ENG, SGG = SGM2[tb], LF2[tb], KK2[tb], GC2[tb], EG2[tb], ENG2[tb], SGG2[tb]
                            qb = 1 if s % 2 == 0 else 6
                            for kc in range(8):
                                MM(PB[0][:], lhsT=WF[:, kc, :], rhs=XN[:, kc, sl(s)], start=(kc == 0), stop=(kc == 7), reads=[('WF', wb), ('XN', kc, s)], writes=[pbr(0)])
                            k.op('act', lambda e: e.activation(out=SGM[:], in_=PB[0][:], func=AF.Sigmoid), reads=[pbr(0)], writes=[('SGM', tb)])
                            for kc in range(8):
                                MM(PB[2][:], lhsT=WGt[:, kc, :], rhs=XN[:, kc, sl(s)], start=(kc == 0), stop=(kc == 7), reads=[('WGt', wb), ('XN', kc, s)], writes=[pbr(2)])
                            k.op('act', lambda e: e.activation(out=SGt[:, sl(s)], in_=PB[2][:], func=AF.Silu), reads=[pbr(2)], writes=[('SGt', hb, s)])
                            k.op('act', lambda e: e.activation(out=LF[:], in_=SGM[:], func=AF.Ln, scale=OML[:, hd:hd + 1], bias=LB[:, hd:hd + 1]),
                                 reads=[('SGM', tb), 'OML', 'LB'], writes=[('LF', tb)])
                            k.op('act', lambda e: e.activation(out=KK[:], in_=SGM[:], func=AF.Identity, scale=NOML[:, hd:hd + 1], bias=OML[:, hd:hd + 1]),
                                 reads=[('SGM', tb), 'NOML', 'OML'], writes=[('KK', tb)])
                            k.op('dve', lambda e: e.tensor_tensor_scan(out=GC[:], data0=CONST[:, 0:512], data1=LF[:], initial=0.0,
                                                                       op0=ALU.mult, op1=ALU.add),
                                 reads=['CONST', ('LF', tb)], writes=[('GC', tb)])
                            k.op('act', lambda e: e.activation(out=EG[:], in_=GC[:], func=AF.Exp), reads=[('GC', tb)], writes=[('EG', tb)])
                            k.op('act', lambda e: e.activation(out=ENG[:], in_=GC[:], func=AF.Exp, scale=-1.0), reads=[('GC', tb)], writes=[('ENG', tb)])
                            k.op('dve', lambda e: e.tensor_copy(out=DK[:, 8 * s:8 * s + 8],
                                                                in_=EG[:].rearrange("p (c s) -> p c s", s=64)[:, :, 63]),
                                 reads=[('EG', tb)], writes=[('DK', hb, s)])
                            for kc in range(8):
                                MM(PB[qb][:], lhsT=WQ[:, kc, :], rhs=XN[:, kc, sl(s)], start=(kc == 0), stop=(kc == 7), reads=[('WQ', wb), ('XN', kc, s)], writes=[pbr(qb)])
                            k.op('dve', lambda e: e.scalar_tensor_tensor(out=QT[:, sl(s)], in0=PB[qb][:], scalar=SCALE, in1=EG[:],
                                                                         op0=ALU.mult, op1=ALU.mult),
                                 reads=[pbr(qb), ('EG', tb)], writes=[('QT', hb, s)])
                            k.op('pool', lambda e: e.tensor_tensor(out=KT[:, sl(s)], in0=KK[:], in1=ENG[:], op=ALU.mult),
                                 reads=[('KK', tb), ('ENG', tb)], writes=[('KT', hb, s)])
                            for t4 in range(4):
                                tk = slice(s * 512 + t4 * 128, s * 512 + (t4 + 1) * 128)
                                for kc in range(8):
                                    MM(PB[3][:, t4 * 128:(t4 + 1) * 128], lhsT=XN[:, kc, tk], rhs=WI[:, kc, :],
                                                                  start=(kc == 0), stop=(kc == 7), reads=[('WI', wb), ('XN', kc, s)], writes=[pbr(3)])
                            k.op('act', lambda e: e.copy(out=VT[:, 4 * s:4 * s + 4, :].rearrange("p a b -> p (a b)"), in_=PB[3][:]),
                                 reads=[pbr(3)], writes=[('VT', hb, s)])

                        def slab_a2(hd, s):
                            hb = hd % NB_
                            KT, KTT = KTs[hb], KTTs[hb]
                            for t4 in range(4):
                                k.op('pe', lambda e: e.transpose(PT7[:, t4 * 128:(t4 + 1) * 128], KT[:, s * 512 + t4 * 128:s * 512 + (t4 + 1) * 128], IDb[:]),
                                     reads=[('KT', hb, s), 'IDb'], writes=[pbr(7)])
                            k.op('act', lambda e: e.copy(out=KTT[:, 4 * s:4 * s + 4, :].rearrange("p a b -> p (a b)"), in_=PT7[:, 0:512]),
                                 reads=[pbr(7)], writes=[('KTT', hb, s)])

                        def pump(g, n):
                            if g is None:
                                return
                            for _ in range(n):
                                try:
                                    next(g)
                                except StopIteration:
                                    return

                        def stage_a(hd, gb):
                            if os.environ.get('NOINTER'):
                                pump(gb, 64)
                            for s in range(NSL):
                                slab_a1(hd, s)
                                if s > 0:
                                    slab_a2(hd, s - 1)
                                pump(gb, (0, 10, 11, 12)[s])
                            slab_a2(hd, NSL - 1)
                            pump(gb, 64)
                            if hd + 1 < 8:
                                load_w(hd + 1)

                        def finish_a(hd):
                            hb = hd % NB_
                            k.op('dve', lambda e: e.tensor_scalar(out=SFINs[hb][:], in0=TPa[1][:], scalar1=DKs[hb][:, 31:32], scalar2=None, op0=ALU.mult),
                                 reads=[('TPa', 1), ('DK', hb, 3)], writes=[('SFIN', hb)])
                            k.dma('sp', cin_h[hd].ap(), SFINs[hb][:], reads=[('SFIN', hb)], writes=[('cin_h', hd)])
                            k.collective(lambda e: e.collective_compute("AllGather", ALU.bypass, replica_groups=PAIRS,
                                                                        ins=[cin_h[hd].ap().opt()], outs=[cout_h[hd].ap().opt()]),
                                         reads=[('cin_h', hd)], writes=[('cout_h', hd)])
                            k.dma('sp', RXHs[hb][:], cout_h[hd].ap()[0:128, :], reads=[('cout_h', hd)], writes=[('RXH', hb)])
                            k.op('pool', lambda e: e.tensor_scalar(out=SINITs[hb][:], in0=RXHs[hb][:], scalar1=FLAG[:, 0:1], scalar2=None, op0=ALU.mult),
                                 reads=[('RXH', hb), 'FLAG'], writes=[('SINIT', hb)])

                        def stage_b(hd, ga):
                            hb = hd % NB_
                            QT, KT, VT, SGt = QTs[hb], KTs[hb], VTs[hb], SGts[hb]
                            if os.environ.get('NOINTER'):
                                pump(ga, 64)
                            for t_i in range(16):
                                pump(ga, 2)
                                s = t_i // 4
                                ob = 2 + s % 2
                                tk = slice(t_i * 128, (t_i + 1) * 128)
                                oc = slice((t_i % 4) * 128, (t_i % 4 + 1) * 128)
                                sb_ = t_i % 2
                                MM(PB[sb_][:, 0:128], lhsT=KT[:, tk], rhs=QT[:, tk], start=True, stop=True, reads=[('KT', hb, s), ('QT', hb, s)], writes=[pbr(sb_)])
                                k.op('dve', lambda e: e.tensor_tensor(out=SC[sb_][:], in0=PB[sb_][:, 0:128], in1=MASKC, op=ALU.mult),
                                     reads=[pbr(sb_), 'CONST'], writes=[('SC', sb_)])
                                MM(PB[ob][:, oc], lhsT=VT[:, t_i, :], rhs=SC[sb_][:], start=True, stop=False, reads=[('VT', hb, s), ('SC', sb_)], writes=[pbr(ob)])
                                for hh in range(2):
                                    c = 2 * t_i + hh
                                    MM(PB[ob][:, t_i % 4 * 128 + 64 * hh:t_i % 4 * 128 + 64 * hh + 64], lhsT=SALL[:, c, :],
                                                                  rhs=QT[:, t_i * 128 + 64 * hh:t_i * 128 + 64 * hh + 64], start=False, stop=(hh == 1), reads=[('SALL', c), ('QT', hb, s)], writes=[pbr(ob)])
                                if t_i % 4 == 3:
                                    k.op('act', lambda e: e.activation(out=OSQ[:], in_=PB[ob][:], func=AF.Square), reads=[pbr(ob)], writes=['OSQ'])
                                    MM(PB[6][:], lhsT=ONES[:], rhs=OSQ[:], start=True, stop=True, reads=['ONES', 'OSQ'], writes=[pbr(6)])
                                    k.op('act', lambda e: e.activation(out=ORS[:], in_=PB[6][:], func=AF.Sqrt, bias=EPS, scale=1.0 / 128),
                                         reads=[pbr(6)], writes=['ORS'])
                                    k.op('dve', lambda e: e.reciprocal(ORS[:], ORS[:]), reads=['ORS'], writes=['ORS'])
                                    k.op('dve', lambda e: e.scalar_tensor_tensor(out=OT[:], in0=PB[ob][:], scalar=GN[:, hd:hd + 1], in1=ORS[:],
                                                                                 op0=ALU.mult, op1=ALU.mult),
                                         reads=[pbr(ob), 'GN', 'ORS'], writes=['OT'])
                                    k.op('pool', lambda e: e.tensor_tensor(out=OG[:, hd, sl(s)], in0=OT[:], in1=SGt[:, sl(s)], op=ALU.mult),
                                         reads=['OT', ('SGt', hb, s)], writes=[('OG', hd, s)])

                        load_w(0)
                        stage_a(0, None)
                        ga = chain(0, None, [], False)
                        next(ga)
                        pump(ga, 64)
                        finish_a(0)
                        for hd in range(8):
                            hb = hd % NB_
                            gb = chain(hb, SINITs[hb][:], [('SINIT', hb)], True)
                            if hd + 1 < 8:
                                stage_a(hd + 1, gb)
                                ga = chain((hd + 1) % NB_, None, [], False)
                                next(ga)
                            else:
                                pump(gb, 64)
                                ga = None
                            stage_b(hd, ga)
                            if hd + 1 < 8:
                                pump(ga, 64)
                                finish_a(hd + 1)
                        _fence(k)

            def ple():
                with ExitStack() as st:
                    norm_to_xn(st, 3)
                    WPG = alloc(st, "WPG", [128, 8, D], BF16); WPP = alloc(st, "WPP", [128, 2, D], BF16)
                    PTb = alloc(st, "PTb", [128, 2, T], BF16)
                    S1 = [alloc(st, "PS1%d" % i, [128, 512]) for i in range(2)]
                    TT = [alloc(st, "PTT%d" % i, [128, 512]) for i in range(2)]
                    k.dma('pool', WPG[:], wpg.rearrange("(kc p) f -> p kc f", p=128), writes=['WPG'])
                    k.dma('pool', WPP[:], wpp.rearrange("(kc p) f -> p kc f", p=128), writes=['WPP'])
                    for s_ in range(NSL):
                        k.dma('pool', PTb[:, :, sl(s_)], pT[:, sl(s_)].rearrange("(kc p) t -> p kc t", p=128), writes=['PTb'])
                    for s in range(NSL):
                        for dc in range(8):
                            b = dc % 2
                            cs = slice(dc * 128, (dc + 1) * 128)
                            for kc in range(8):
                                MM(PB[b][:], lhsT=WPG[:, kc, cs], rhs=XN[:, kc, sl(s)], start=(kc == 0), stop=(kc == 7), reads=['WPG', ('XN', kc, s)], writes=[pbr(b)])
                            for kc in range(2):
                                MM(PB[2 + b][:], lhsT=WPP[:, kc, cs], rhs=PTb[:, kc, sl(s)], start=(kc == 0), stop=(kc == 1), reads=['WPP', 'PTb'], writes=[pbr(2 + b)])
                            k.op('act', lambda e: e.activation(out=S1[b][:], in_=PB[b][:], func=AF.Sigmoid), reads=[pbr(b)], writes=[('PS1', b)])
                            k.op('dve', lambda e: e.tensor_tensor(out=TT[b][:], in0=S1[b][:], in1=PB[2 + b][:], op=ALU.mult),
                                 reads=[('PS1', b), pbr(2 + b)], writes=[('PTT', b)])
                            k.op('dve', lambda e: e.tensor_tensor(out=Hh[0][:, dc, sl(s)], in0=TT[b][:], in1=Hh[0][:, dc, sl(s)], op=ALU.add),
                                 reads=[('PTT', b), ('H', dc, s)], writes=[('H', dc, s)])
                    _fence(k)

            if 'ffn1' in stages:
                ffn(0, w1g, w1u, w1d)
            if 's5' in stages or 'hg' in stages:
                with ExitStack() as st:
                    norm_to_xn(st, 1)
                    _fence(k)
                if 's5' in stages:
                    s5_branch()
                if 'hg' in stages:
                    hres = [('H', kc, s_) for kc in range(8) for s_ in range(NSL)]
                    if 's5' not in stages:
                        for s_ in range(NSL):
                            spill_slab(s_)
                    _fence(k)
                    hst[0].close()
                    sB = ExitStack()
                    sBh.append(sB)
                    OG = alloc(sB, "OG", [128, 8, T], BF16)
                    hg_core(OG)
                    hst[0] = ExitStack()
                    Hh[0] = alloc(hst[0], "H", [128, 8, T])
                    for kc in range(8):
                        k.dma('sp', Hh[0][:, kc, :], hsp.ap()[:, kc, :], reads=[('hsp', kc, s_) for s_ in range(NSL)], writes=[('H', kc, s_) for s_ in range(NSL)])
                    with ExitStack() as stl:
                        tail(stl, 8, lambda kc, s: OG[:, kc, sl(s)], lambda kc, s: ('OG', kc, s), hgwo, None, W_GB)
            if 'ffn2' in stages:
                ffn(2, w2g, w2u, w2d)
            if 'ple' in stages:
                ple()

            with ExitStack() as st:
                if final_norm:
                    def emit(s, kc, RS):
                        k.op('dve', lambda e: e.scalar_tensor_tensor(out=Hh[0][:, kc, sl(s)], in0=Hh[0][:, kc, sl(s)],
                                                                     scalar=G[:, 4, kc:kc + 1], in1=RS[:],
                                                                     op0=ALU.mult, op1=ALU.mult),
                             reads=[('H', kc, s), 'RS', 'G'], writes=[('H', kc, s)])
                        if kc == 7:
                            for c2 in range(8):
                                k.dma('sp', outT[c2 * 128:(c2 + 1) * 128, sl(s)], Hh[0][:, c2, sl(s)],
                                      reads=[('H', c2, s)], writes=[('OUT', c2, s)])
                    rmsnorm(st, emit)
                else:
                    for s in range(NSL):
                        for c2 in range(8):
                            k.dma('sp', outT[c2 * 128:(c2 + 1) * 128, sl(s)], Hh[0][:, c2, sl(s)],
                                  reads=[('H', c2, s)], writes=[('OUT', c2, s)])
                k.finish('sp', [('OUT', c2, s) for c2 in range(8) for s in range(NSL)])
            hst[0].close()
            for sb_ in sBh:
                sb_.close()
    return nc


def _s5_layouts(inputs):
    lre = np.asarray(inputs['s5_lam_re'][0], np.float32); lim = np.asarray(inputs['s5_lam_im'][0], np.float32)
    ldt = np.asarray(inputs['s5_log_dt'][0], np.float32)
    bre = np.asarray(inputs['s5_b_re'][0], np.float32); bim = np.asarray(inputs['s5_b_im'][0], np.float32)
    cre = np.asarray(inputs['s5_c_re'][0], np.float32); cim = np.asarray(inputs['s5_c_im'][0], np.float32)
    w_in = np.asarray(inputs['w_in'][0], np.float32)
    gv = np.asarray(inputs['s5_glu_val'][0], np.float32); gg = np.asarray(inputs['s5_glu_gate'][0], np.float32)
    d5 = np.asarray(inputs['s5_d'][0], np.float32)
    LT = np.zeros((128, 3, 6, 128), np.float32); BT = np.zeros((128, 2, 6, 128), np.float32)
    W5 = np.zeros((D, 6, 128), np.float32); GV = np.zeros((6, 128, D), np.float32); GG = np.zeros((6, 128, D), np.float32)
    D5 = np.zeros((128, 6), np.float32)
    for q in range(6):
        for b in range(3):
            P = min(3 * q + b, 15)
            valid = (3 * q + b) < 16
            for g2p in range(2):
                g = 2 * P + g2p
                cs = slice(g2p * 64, (g2p + 1) * 64)
                LT[:, 0, q, cs][32 * b:32 * b + 32] = lre[g][None, :]
                LT[:, 1, q, cs][32 * b:32 * b + 32] = lim[g][None, :]
                LT[:, 2, q, cs][32 * b:32 * b + 32] = ldt[g]
                if valid:
                    ps = slice(32 * b + 16 * g2p, 32 * b + 16 * g2p + 16)
                    BT[ps, 0, q, cs] = bre[g].T
                    BT[ps, 1, q, cs] = bim[g].T
            if valid:
                W5[:, q, 32 * b:32 * b + 32] = w_in[:, 32 * P:32 * P + 32]
                GV[q, 32 * b:32 * b + 32, :] = gv[32 * P:32 * P + 32, :]
                GG[q, 32 * b:32 * b + 32, :] = gg[32 * P:32 * P + 32, :]
                D5[32 * b:32 * b + 32, q] = d5[32 * P:32 * P + 32]
    LT[96:128, 0] = LT[0:32, 0]; LT[96:128, 1] = LT[0:32, 1]; LT[96:128, 2] = LT[0:32, 2]
    LCc = np.zeros((128, 3, 16), np.float32); CC = np.zeros((128, 2, 16, 32), np.float32)
    for P in range(16):
        for g2 in range(2):
            g = 2 * P + g2
            ps = slice(64 * g2, 64 * g2 + 64)
            LCc[ps, 0, P] = lre[g]; LCc[ps, 1, P] = lim[g]; LCc[ps, 2, P] = ldt[g]
            CC[ps, 0, P, 16 * g2:16 * g2 + 16] = cre[g].T
            CC[ps, 1, P, 16 * g2:16 * g2 + 16] = cim[g].T
    return (LT.reshape(128, 3, 768), BT.reshape(128, 2, 768), LCc, CC.reshape(128, 2, 512), D5,
            W5.reshape(D, 768), GV.reshape(768, D), GG.reshape(768, D))


def make_in_maps(inputs):
    f = lambda a: np.ascontiguousarray(np.asarray(a, dtype=np.float32))
    x = np.asarray(inputs['x'], np.float32)
    p = np.asarray(inputs['p'], np.float32)
    gl = lambda v: np.asarray(v, np.float32).reshape(8, 128).T
    gains = np.stack([gl(inputs['ffn1_norm'][0]), gl(inputs['mix_norm'][0]), gl(inputs['ffn2_norm'][0]),
                      gl(inputs['ple_norm'][0]), gl(inputs['final_norm']), gl(inputs['final_norm'])], axis=1)
    consts = np.zeros((128, 768), np.float32)
    consts[:, 0:512] = 1.0
    consts[:, 0:512:64] = 0.0
    ii = np.arange(128)
    consts[:, 512:640] = ((ii[None, :] >= ii[:, None]) & ((ii[None, :] // 64) == (ii[:, None] // 64))).astype(np.float32)
    consts[:, 640:768] = np.eye(128, dtype=np.float32)
    LT, BT, LCc, CC, D5, W5h, GVh, GGh = _s5_layouts(inputs)
    hglb = np.asarray(inputs['hg_lower_bound'], np.float32).reshape(2, 8, 128).transpose(2, 0, 1)
    shared = {
        'gains': f(gains), 'consts': consts,
        'ffn1_w_gate': f(inputs['ffn1_w_gate'][0]), 'ffn1_w_up': f(inputs['ffn1_w_up'][0]),
        'ffn1_w_down': f(inputs['ffn1_w_down'][0]),
        'ffn2_w_gate': f(inputs['ffn2_w_gate'][0]), 'ffn2_w_up': f(inputs['ffn2_w_up'][0]),
        'ffn2_w_down': f(inputs['ffn2_w_down'][0]),
        'ple_w_gate': f(inputs['ple_w_gate'][0]), 'ple_w_proj': f(inputs['ple_w_proj'][0]),
        'w_in': f(inputs['w_in'][0]),
        's5_LT': f(LT), 's5_BT': f(BT), 's5_LCc': f(LCc), 's5_CC': f(CC), 's5_D5': f(D5), 's5_w5': f(W5h),
        's5_glu_val': f(GVh), 's5_glu_gate': f(GGh),
        'hg_lb': f(hglb), 'hg_gn': f(gl(inputs['hg_out_norm'][0])),
        'hg_w_out': f(inputs['hg_w_out'][0]), 'w_merge_out': f(inputs['w_merge_out'][0]),
    }
    maps = []
    for c in range(NCORES):
        b, hf = c // 2, c % 2
        m = dict(shared)
        m['xT'] = f(x[b, hf * T:(hf + 1) * T, :].T)
        m['pT'] = f(p[0, b, hf * T:(hf + 1) * T, :].T)
        m['flag'] = np.full((128, 1), float(hf), np.float32)
        maps.append(m)
    return maps


def assemble(results):
    out = np.empty((4, 4096, D), np.float32)
    for c in range(NCORES):
        b, hf = c // 2, c % 2
        out[b, hf * T:(hf + 1) * T, :] = results[c]["outT"].T
    return out


_NC_CACHE = {}


def kernel(**inputs):
    if 'nc' not in _NC_CACHE:
        _NC_CACHE['nc'] = build()
    nc = _NC_CACHE['nc']
    res = run_bass_kernel_spmd(nc, make_in_maps(inputs), core_ids=list(range(NCORES)))
    return assemble(res.results)
```

```python
import os
import numpy as np
from contextlib import ExitStack
import concourse.bass as bass
import concourse.mybir as mybir
from concourse.bass_utils import run_bass_kernel_spmd

F32 = mybir.dt.float32
BF16 = mybir.dt.bfloat16
AF = mybir.ActivationFunctionType
ALU = mybir.AluOpType

NCORES = 8
T = 2048
D = 1024
DFF = 2816
SL = 512
NSL = T // SL
EPS = 1e-6
ND = 24


class K:
    def __init__(self, nc, es):
        self.nc = nc
        self.es = es
        self.eng = {'pe': nc.tensor, 'act': nc.scalar, 'dve': nc.vector, 'pool': nc.gpsimd, 'sp': nc.sync}
        self.sem = {e: es.enter_context(nc.semaphore("sem_" + e)) for e in self.eng}
        self.cnt = {e: 0 for e in self.eng}
        self.waited = {e: {} for e in self.eng}
        self.dsems = [es.enter_context(nc.semaphore("dsem%d" % i)) for i in range(ND)]
        self.dcnt = [0] * ND
        self.dnext = 0
        self.csem = es.enter_context(nc.semaphore("ccsem"))
        self.ccnt = 0
        self.res = {}

    def _semof(self, key):
        if key[0] == 'e':
            return self.sem[key[1]]
        if key[0] == 'd':
            return self.dsems[key[1]]
        return self.csem

    def _wait(self, e, key, val):
        if key == ('e', 'pe') and e == 'pe':
            return
        if self.waited[e].get(key, 0) >= val:
            return
        self.eng[e].wait_ge(self._semof(key), val)
        self.waited[e][key] = val

    def _deps(self, e, reads, writes):
        evs = {}

        def add(ev):
            if ev is None:
                return
            k, v = ev
            if evs.get(k, 0) < v:
                evs[k] = v
        for r in reads:
            st = self.res.get(r)
            if st:
                add(st['w'])
        for w in writes:
            st = self.res.get(w)
            if st:
                add(st['w'])
                for k, v in st['r'].items():
                    add((k, v))
        for k, v in evs.items():
            self._wait(e, k, v)

    def _commit(self, ev, reads, writes):
        k, v = ev
        for r in reads:
            st = self.res.setdefault(r, {'w': None, 'r': {}})
            if st['r'].get(k, 0) < v:
                st['r'][k] = v
        for w in writes:
            self.res[w] = {'w': ev, 'r': {}}

    def op(self, e, fn, reads=(), writes=(), sig=True):
        self._deps(e, reads, writes)
        inst = fn(self.eng[e])
        if not sig:
            self._commit((('e', e), self.cnt[e] + 1), reads, writes)
            return
        self.cnt[e] += 1
        inst.then_inc(self.sem[e], 1)
        self._commit((('e', e), self.cnt[e]), reads, writes)

    def dma(self, q, out, in_, reads=(), writes=(), **kw):
        i = self.dnext
        self.dnext = (i + 1) % ND
        if self.dcnt[i] > 0:
            self._wait(q, ('d', i), self.dcnt[i])
        self._deps(q, reads, writes)
        self.dcnt[i] += 16
        self.eng[q].dma_start(out=out, in_=in_, **kw).then_inc(self.dsems[i], 16)
        self._commit((('d', i), self.dcnt[i]), reads, writes)

    def collective(self, fn, reads=(), writes=()):
        self._deps('pool', reads, writes)
        self.ccnt += 1
        fn(self.eng['pool']).then_inc(self.csem)
        self._commit((('c', 0), self.ccnt), reads, writes)

    def finish(self, e, names):
        self._deps(e, names, ())


def _fence(k):
    for e in k.eng:
        for f in k.eng:
            if f != e and k.cnt[f] > 0:
                k._wait(e, ('e', f), k.cnt[f])
        for i in range(ND):
            if k.dcnt[i] > 0:
                k._wait(e, ('d', i), k.dcnt[i])
        if k.ccnt > 0:
            k._wait(e, ('c', 0), k.ccnt)


W_S5, W_Q, W_F, W_I, W_G, W_GA, W_GB = 0, 512, 1536, 2560, 3584, 4608, 5632
TWO_PI = 6.283185307179586
CW1 = 6.28125
CW2 = TWO_PI - CW1
GELU_K = 1.5957691216057308


def build(stages=('ffn1', 's5', 'hg', 'ffn2', 'ple'), final_norm=True, stop=None):
    nc = bass.Bass("TRN2", target_bir_lowering=False)

    def din(name, shape):
        return nc.dram_tensor(name, list(shape), F32, kind="ExternalInput").ap()

    xT = din("xT", [D, T])
    pT = din("pT", [256, T])
    gains = din("gains", [128, 6, 8])
    flag_d = din("flag", [128, 1])
    consts = din("consts", [128, 768])
    w1g = din("ffn1_w_gate", [D, DFF]); w1u = din("ffn1_w_up", [D, DFF]); w1d = din("ffn1_w_down", [DFF, D])
    w2g = din("ffn2_w_gate", [D, DFF]); w2u = din("ffn2_w_up", [D, DFF]); w2d = din("ffn2_w_down", [DFF, D])
    wpg = din("ple_w_gate", [D, D]); wpp = din("ple_w_proj", [256, D])
    w_in = din("w_in", [D, 6656])
    LT_d = din("s5_LT", [128, 3, 768]); BT_d = din("s5_BT", [128, 2, 768]); w5_d = din("s5_w5", [D, 768])
    LCc_d = din("s5_LCc", [128, 3, 16]); CC_d = din("s5_CC", [128, 2, 512]); D5_d = din("s5_D5", [128, 6])
    glv = din("s5_glu_val", [768, D]); glg = din("s5_glu_gate", [768, D])
    hglb_d = din("hg_lb", [128, 2, 8]); hggn_d = din("hg_gn", [128, 8])
    hgwo = din("hg_w_out", [D, D]); wmo = din("w_merge_out", [D, D])
    outT = nc.dram_tensor("outT", [D, T], F32, kind="ExternalOutput").ap()
    cin_s = nc.dram_tensor("cin_s", [128, 32], F32); cout_s = nc.dram_tensor("cout_s", [256, 32], F32)
    cin_h = [nc.dram_tensor("cin_h%d" % i, [128, 128], F32) for i in range(8)]
    cout_h = [nc.dram_tensor("cout_h%d" % i, [256, 128], F32) for i in range(8)]
    PAIRS = [[0, 1], [2, 3], [4, 5], [6, 7]]
    NBQ = [3, 3, 3, 3, 3, 1]

    with ExitStack() as es:
        _uid = [0]

        def alloc(st, name, shape, dt=F32):
            _uid[0] += 1
            return st.enter_context(nc.sbuf_tensor("%s_%d" % (name, _uid[0]), list(shape), dt))

        XN = alloc(es, "XN", [128, 8, T], BF16)
        G = alloc(es, "G", [128, 6, 8])
        ONES = alloc(es, "ONES", [128, 128], BF16)
        FLAG = alloc(es, "FLAG", [128, 1])
        PB = [es.enter_context(nc.psum_tensor("PB%d" % i, [128, 512], F32)) for i in range(8)]
        hst = [ExitStack()]
        sBh = []
        Hh = [alloc(hst[0], "H", [128, 8, T])]
        hsp = nc.dram_tensor("h_spill", [128, 8, T], F32)

        block = es.enter_context(nc.Block())

        @block.sync
        def _(sync):
            k = K(nc, es)
            sl = lambda s: slice(s * SL, (s + 1) * SL)

            def MM(out, lhsT, rhs, start, stop, reads, writes, sig=None):
                k.op('pe', lambda e: e.matmul(out, lhsT=lhsT, rhs=rhs, start=start, stop=stop), reads=reads, writes=writes,
                     sig=bool(stop) if sig is None else sig)
            pbr = lambda i: ('PB', i)

            for s_ in range(NSL):
                for kc in range(8):
                    k.dma('sp', Hh[0][:, kc, sl(s_)], xT[kc * 128:(kc + 1) * 128, sl(s_)], writes=[('H', kc, s_)])
            k.dma('sp', G[:], gains, writes=['G'])
            k.dma('sp', FLAG[:], flag_d, writes=['FLAG'])
            k.op('dve', lambda e: e.memset(ONES[:], 1.0), writes=['ONES'])

            def rmsnorm(st, emit, src=None, nparts=D, tag='n'):
                SQ = [alloc(st, "SQ%s%d" % (tag, i), [128, 512], BF16) for i in range(2)]
                RS = alloc(st, "RS" + tag, [128, 512])
                for s in range(NSL):
                    for kc in range(8):
                        q = SQ[kc % 2]
                        k.op('act', lambda e: e.activation(out=q[:], in_=Hh[0][:, kc, sl(s)], func=AF.Square),
                             reads=[('H', kc, s)], writes=[('SQ', kc % 2)])
                        MM(PB[6][:], lhsT=ONES[:], rhs=q[:], start=(kc == 0), stop=(kc == 7), reads=[('SQ', kc % 2), 'ONES'], writes=[pbr(6)], sig=True)
                    k.op('act', lambda e: e.activation(out=RS[:], in_=PB[6][:], func=AF.Sqrt, bias=EPS, scale=1.0 / D),
                         reads=[pbr(6)], writes=['RS'])
                    k.op('dve', lambda e: e.reciprocal(RS[:], RS[:]), reads=['RS'], writes=['RS'])
                    for kc in range(8):
                        emit(s, kc, RS)

            def norm_to_xn(st, which):
                def emit(s, kc, RS):
                    k.op('dve', lambda e: e.scalar_tensor_tensor(out=XN[:, kc, sl(s)], in0=Hh[0][:, kc, sl(s)],
                                                                 scalar=G[:, which, kc:kc + 1], in1=RS[:],
                                                                 op0=ALU.mult, op1=ALU.mult),
                         reads=[('H', kc, s), 'RS', 'G'], writes=[('XN', kc, s)])
                rmsnorm(st, emit)

            def ffn(which, wg, wu, wd):
                with ExitStack() as st:
                    norm_to_xn(st, which)
                    WG = [alloc(st, "WG%d" % i, [128, 8, 512], BF16) for i in range(2)]
                    WU = [alloc(st, "WU%d" % i, [128, 8, 512], BF16) for i in range(2)]
                    WD = [alloc(st, "WD%d" % i, [128, 4, D], BF16) for i in range(2)]
                    HM = [alloc(st, "HM%d" % i, [128, 512], BF16) for i in range(8)]
                    SG = [alloc(st, "SG%d" % i, [128, 512]) for i in range(2)]
                    pieces = [(0, 512), (512, 512), (1024, 512), (1536, 512), (2048, 512), (2560, 256)]
                    it = 0
                    for j, (f0, fw) in enumerate(pieces):
                        b = j % 2
                        nfc = fw // 128
                        k.dma('pool', WG[b][:, :, :fw], wg[:, f0:f0 + fw].rearrange("(kc p) f -> p kc f", p=128), writes=[('WG', b)])
                        k.dma('pool', WU[b][:, :, :fw], wu[:, f0:f0 + fw].rearrange("(kc p) f -> p kc f", p=128), writes=[('WU', b)])
                        k.dma('pool', WD[b][:, :nfc, :], wd[f0:f0 + fw, :].rearrange("(fc p) d -> p fc d", p=128), writes=[('WD', b)])
                        for s in range(NSL):
                            hs = (s % 2) * 4
                            for fc in range(nfc):
                                pb = it % 2
                                it += 1
                                for kc in range(8):
                                    MM(PB[pb][:], lhsT=WG[b][:, kc, fc * 128:(fc + 1) * 128],
                                                                  rhs=XN[:, kc, sl(s)], start=(kc == 0), stop=(kc == 7), reads=[('WG', b), ('XN', kc, s)], writes=[pbr(pb)])
                                for kc in range(8):
                                    MM(PB[2 + pb][:], lhsT=WU[b][:, kc, fc * 128:(fc + 1) * 128],
                                                                  rhs=XN[:, kc, sl(s)], start=(kc == 0), stop=(kc == 7), reads=[('WU', b), ('XN', kc, s)], writes=[pbr(2 + pb)])
                                k.op('act', lambda e: e.activation(out=SG[pb][:], in_=PB[pb][:], func=AF.Silu),
                                     reads=[pbr(pb)], writes=[('SG', pb)])
                                k.op('dve', lambda e: e.tensor_tensor(out=HM[hs + fc][:], in0=SG[pb][:], in1=PB[2 + pb][:], op=ALU.mult),
                                     reads=[('SG', pb), pbr(2 + pb)], writes=[('HM', hs + fc)])
                            for dc in range(8):
                                pb = 4 + dc % 2
                                for fc in range(nfc):
                                    MM(PB[pb][:], lhsT=WD[b][:, fc, dc * 128:(dc + 1) * 128],
                                                                  rhs=HM[hs + fc][:], start=(fc == 0), stop=(fc == nfc - 1), reads=[('WD', b), ('HM', hs + fc)], writes=[pbr(pb)])
                                k.op('dve', lambda e: e.scalar_tensor_tensor(out=Hh[0][:, dc, sl(s)], in0=PB[pb][:], scalar=0.5,
                                                                             in1=Hh[0][:, dc, sl(s)], op0=ALU.mult, op1=ALU.add),
                                     reads=[pbr(pb), ('H', dc, s)], writes=[('H', dc, s)])
                    _fence(k)

            def tail(st, nk, src_fn, src_res, wval_d, wgat_d, gate_col, after_slab=None):
                WV = alloc(st, "WV", [128, nk, D], BF16)
                WGT = alloc(st, "WGT", [128, nk, D], BF16) if wgat_d is not None else None
                WA = alloc(st, "WA", [128, 8, D], BF16)
                WM = alloc(st, "WM", [128, 8, D], BF16)
                MX = [alloc(st, "MX%d" % i, [128, 512], BF16) for i in range(16)]
                S1 = [alloc(st, "S1%d" % i, [128, 512]) for i in range(2)]
                S2 = [alloc(st, "S2%d" % i, [128, 512]) for i in range(2)]
                TT = [alloc(st, "TT%d" % i, [128, 512]) for i in range(2)]
                k.dma('pool', WV[:], wval_d.rearrange("(kc p) f -> p kc f", p=128), writes=['WV'])
                if WGT is not None:
                    k.dma('pool', WGT[:], wgat_d.rearrange("(kc p) f -> p kc f", p=128), writes=['WGT'])
                k.dma('pool', WA[:], w_in[:, gate_col:gate_col + D].rearrange("(kc p) f -> p kc f", p=128), writes=['WA'])
                k.dma('pool', WM[:], wmo.rearrange("(kc p) f -> p kc f", p=128), writes=['WM'])
                for s in range(NSL):
                    ms = (s % 2) * 8
                    for dc in range(8):
                        b = dc % 2
                        cs = slice(dc * 128, (dc + 1) * 128)
                        for kc in range(nk):
                            MM(PB[b][:], lhsT=WV[:, kc, cs], rhs=src_fn(kc, s),
                                                          start=(kc == 0), stop=(kc == nk - 1), reads=['WV', src_res(kc, s)], writes=[pbr(b)])
                        if WGT is not None:
                            for kc in range(nk):
                                MM(PB[2 + b][:], lhsT=WGT[:, kc, cs], rhs=src_fn(kc, s),
                                                              start=(kc == 0), stop=(kc == nk - 1), reads=['WGT', src_res(kc, s)], writes=[pbr(2 + b)])
                        for kc in range(8):
                            MM(PB[4 + b][:], lhsT=WA[:, kc, cs], rhs=XN[:, kc, sl(s)],
                                                          start=(kc == 0), stop=(kc == 7), reads=['WA', ('XN', kc, s)], writes=[pbr(4 + b)])
                        k.op('act', lambda e: e.activation(out=S2[b][:], in_=PB[4 + b][:], func=AF.Sigmoid),
                             reads=[pbr(4 + b)], writes=[('S2', b)])
                        if WGT is not None:
                            k.op('act', lambda e: e.activation(out=S1[b][:], in_=PB[2 + b][:], func=AF.Sigmoid),
                                 reads=[pbr(2 + b)], writes=[('S1', b)])
                            k.op('dve', lambda e: e.tensor_tensor(out=TT[b][:], in0=S1[b][:], in1=PB[b][:], op=ALU.mult),
                                 reads=[('S1', b), pbr(b)], writes=[('TT', b)])
                            k.op('dve', lambda e: e.tensor_tensor(out=MX[ms + dc][:], in0=TT[b][:], in1=S2[b][:], op=ALU.mult),
                                 reads=[('TT', b), ('S2', b)], writes=[('MX', ms + dc)])
                        else:
                            k.op('dve', lambda e: e.tensor_tensor(out=MX[ms + dc][:], in0=S2[b][:], in1=PB[b][:], op=ALU.mult),
                                 reads=[('S2', b), pbr(b)], writes=[('MX', ms + dc)])
                    for d2 in range(8):
                        b = 6 + d2 % 2
                        for dc in range(8):
                            MM(PB[b][:], lhsT=WM[:, dc, d2 * 128:(d2 + 1) * 128], rhs=MX[ms + dc][:],
                                                          start=(dc == 0), stop=(dc == 7), reads=['WM', ('MX', ms + dc)], writes=[pbr(b)])
                        k.op('dve', lambda e: e.tensor_tensor(out=Hh[0][:, d2, sl(s)], in0=PB[b][:], in1=Hh[0][:, d2, sl(s)], op=ALU.add),
                             reads=[pbr(b), ('H', d2, s)], writes=[('H', d2, s)])
                    if after_slab is not None:
                        after_slab(s)
                _fence(k)

            PE_ = 'dve'

            def tt(out, a, b, op, r, w):
                k.op(PE_, lambda e: e.tensor_tensor(out=out, in0=a, in1=b, op=op), reads=r, writes=w)

            def ts(out, a, s1, s2, op0, op1, r, w):
                if s2 is None:
                    k.op(PE_, lambda e: e.tensor_scalar(out=out, in0=a, scalar1=s1, scalar2=None, op0=op0), reads=r, writes=w)
                else:
                    k.op(PE_, lambda e: e.tensor_scalar(out=out, in0=a, scalar1=s1, scalar2=s2, op0=op0, op1=op1), reads=r, writes=w)

            def lam_l1(st, tag, LRE, LIM, LDT, W, rin):
                L1R = alloc(st, tag + "_l1r", [128, W]); L1I = alloc(st, tag + "_l1i", [128, W])
                sti = ExitStack()

                def t_(n, dt=F32):
                    return alloc(sti, "%s_%s" % (tag, n), [128, W], dt)
                DTt, A, TH, MAG, KF, HS, HC = [t_(n) for n in ("dt", "a", "th", "mag", "kf", "hs", "hc")]
                KI = t_("ki", mybir.dt.int32)
                n = lambda x: (tag, x)
                k.op('act', lambda e: e.activation(out=DTt[:], in_=LDT, func=AF.Exp), reads=rin, writes=[n('dt')])
                tt(A[:], LRE, DTt[:], ALU.mult, rin + [n('dt')], [n('a')])
                tt(TH[:], LIM, DTt[:], ALU.mult, rin + [n('dt')], [n('th')])
                k.op('act', lambda e: e.activation(out=MAG[:], in_=A[:], func=AF.Exp), reads=[n('a')], writes=[n('mag')])
                ts(KF[:], TH[:], 1.0 / TWO_PI, None, ALU.mult, None, [n('th')], [n('kf')])
                k.op(PE_, lambda e: e.tensor_copy(out=KI[:], in_=KF[:]), reads=[n('kf')], writes=[n('ki')])
                k.op(PE_, lambda e: e.tensor_copy(out=KF[:], in_=KI[:]), reads=[n('ki')], writes=[n('kf')])
                ts(A[:], KF[:], -CW1, None, ALU.mult, None, [n('kf'), n('mag')], [n('a')])
                tt(TH[:], TH[:], A[:], ALU.add, [n('th'), n('a')], [n('th')])
                ts(A[:], KF[:], -CW2, None, ALU.mult, None, [n('kf')], [n('a')])
                tt(TH[:], TH[:], A[:], ALU.add, [n('th'), n('a')], [n('th')])
                k.op('act', lambda e: e.activation(out=HS[:], in_=TH[:], func=AF.Sin, scale=0.5), reads=[n('th')], writes=[n('hs')])
                k.op('act', lambda e: e.activation(out=HC[:], in_=TH[:], func=AF.Sin, scale=-0.5, bias=float(np.pi / 2)),
                     reads=[n('th')], writes=[n('hc')])
                tt(A[:], HS[:], HC[:], ALU.mult, [n('hs'), n('hc')], [n('a')])
                ts(A[:], A[:], 2.0, None, ALU.mult, None, [n('a')], [n('a')])
                tt(L1I[:], A[:], MAG[:], ALU.mult, [n('a'), n('mag')], [n('l1i')])
                tt(A[:], HS[:], HS[:], ALU.mult, [n('hs')], [n('a')])
                ts(A[:], A[:], -2.0, 1.0, ALU.mult, ALU.add, [n('a')], [n('a')])
                tt(L1R[:], A[:], MAG[:], ALU.mult, [n('a'), n('mag')], [n('l1r')])
                _fence(k)
                sti.close()
                return L1R, L1I

            def cmul(outr, outi, ar, ai, br, bi, tmp, r, w, tag):
                t1, t2 = tmp
                tr = [(tag, 'ct1')]
                ti_ = [(tag, 'ct2')]
                tt(t1, ar, br, ALU.mult, r, tr)
                tt(t2, ai, bi, ALU.mult, r, ti_)
                tt(outr, t1, t2, ALU.subtract, tr + ti_, [w[0]])
                tt(t1, ar, bi, ALU.mult, r + [w[0]], tr)
                tt(t2, ai, br, ALU.mult, r + [w[0]], ti_)
                tt(outi, t1, t2, ALU.add, tr + ti_, [w[1]])

            def spill_slab(s):
                if 'hg' not in stages:
                    return
                for kc in range(8):
                    k.dma('sp', hsp.ap()[:, kc, sl(s)], Hh[0][:, kc, sl(s)], reads=[('H', kc, s)], writes=[('hsp', kc, s)])

            def s5_branch():
                with ExitStack() as sA:
                    U5 = alloc(sA, "U5", [128, 6, T], BF16)
                    with ExitStack() as s5:
                        X1B = alloc(s5, "X1B", [128, 16, 2, 258], BF16)
                        WIN = alloc(s5, "WIN", [128, 8, 2, 768], BF16)
                        LAM = alloc(s5, "LAM", [128, 9, 2, 16])
                        LR2 = alloc(s5, "LR2", [128, 16, 2]); NLI = alloc(s5, "NLI", [128, 16])
                        XINIT = alloc(s5, "XINIT", [128, 16, 2])
                        LAMB = alloc(s5, "LAMB", [128, 17, 2, 16]); NLB = alloc(s5, "NLB", [128, 17, 16]); L16 = alloc(s5, "L16", [128, 16, 2])
                        D5 = alloc(s5, "D5", [128, 6])
                        k.dma('sp', D5[:], D5_d, writes=['D5'])
                        for hf_ in range(2):
                          with ExitStack() as sp_:
                            wc = slice(hf_ * 384, (hf_ + 1) * 384)
                            LT = alloc(sp_, "LT", [128, 3, 384]); BT = alloc(sp_, "BT", [128, 2, 384])
                            k.dma('sp', LT[:], LT_d[:, :, wc], writes=['LT'])
                            k.dma('sp', BT[:], BT_d[:, :, wc], writes=['BT'])
                            L1R, L1I = lam_l1(sp_, "pt", LT[:, 0, :], LT[:, 1, :], LT[:, 2, :], 384, ['LT'])
                            nm = lambda x: ('pt', x)
                            tl = lambda n_: alloc(sp_, "pt_" + n_, [128, 384])
                            NR, D2, T1, T2, CR, CI, BBR, BBI = [tl(x) for x in ("nr", "d2", "t1", "t2", "cr", "ci", "bbr", "bbi")]
                            PR = [tl("pr0"), tl("pr1")]; PI = [tl("pi0"), tl("pi1")]
                            ts(NR[:], L1R[:], -1.0, None, ALU.add, None, [nm('l1r')], [nm('nr')])
                            tt(D2[:], LT[:, 0, :], LT[:, 0, :], ALU.mult, ['LT'], [nm('d2')])
                            tt(T1[:], LT[:, 1, :], LT[:, 1, :], ALU.mult, ['LT'], [nm('t1')])
                            tt(D2[:], D2[:], T1[:], ALU.add, [nm('d2'), nm('t1')], [nm('d2')])
                            k.op('dve', lambda e: e.reciprocal(D2[:], D2[:]), reads=[nm('d2')], writes=[nm('d2')])
                            tt(CR[:], NR[:], LT[:, 0, :], ALU.mult, [nm('nr'), 'LT'], [nm('cr')])
                            tt(T1[:], L1I[:], LT[:, 1, :], ALU.mult, [nm('l1i'), 'LT', nm('d2')], [nm('t1')])
                            tt(CR[:], CR[:], T1[:], ALU.add, [nm('cr'), nm('t1')], [nm('cr')])
                            tt(CR[:], CR[:], D2[:], ALU.mult, [nm('cr'), nm('d2')], [nm('cr')])
                            tt(CI[:], L1I[:], LT[:, 0, :], ALU.mult, [nm('l1i'), 'LT'], [nm('ci')])
                            tt(T1[:], NR[:], LT[:, 1, :], ALU.mult, [nm('nr'), 'LT', nm('cr')], [nm('t1')])
                            tt(CI[:], CI[:], T1[:], ALU.subtract, [nm('ci'), nm('t1')], [nm('ci')])
                            tt(CI[:], CI[:], D2[:], ALU.mult, [nm('ci'), nm('d2')], [nm('ci')])
                            cmul(BBR[:], BBI[:], CR[:], CI[:], BT[:, 0, :], BT[:, 1, :], (T1[:], T2[:]),
                                 [nm('cr'), nm('ci'), 'BT'], [nm('bbr'), nm('bbi')], 'pt')
                            k.op(PE_, lambda e: e.tensor_copy(out=WIN[:, 0, 0, wc], in_=BBR[:]), reads=[nm('bbr')], writes=[('WIN', 0, 0)])
                            k.op(PE_, lambda e: e.tensor_copy(out=WIN[:, 0, 1, wc], in_=BBI[:]), reads=[nm('bbi')], writes=[('WIN', 0, 1)])
                            cur_r, cur_i, rr = L1R[:], L1I[:], [nm('l1r'), nm('l1i')]
                            for j in range(1, 8):
                                if j > 1:
                                    pp = j % 2
                                    cmul(PR[pp][:], PI[pp][:], cur_r, cur_i, L1R[:], L1I[:], (T1[:], T2[:]),
                                         rr + [nm('l1r'), nm('l1i')], [nm(('pr', pp)), nm(('pi', pp))], 'pt')
                                    cur_r, cur_i, rr = PR[pp][:], PI[pp][:], [nm(('pr', pp)), nm(('pi', pp))]
                                cmul(WIN[:, j, 0, wc], WIN[:, j, 1, wc], cur_r, cur_i, BBR[:], BBI[:], (T1[:], T2[:]),
                                     rr + [nm('bbr'), nm('bbi')], [('WIN', j, 0), ('WIN', j, 1)], 'pt')
                            _fence(k)
                        if stop == 'prepT':
                            return
                        with ExitStack() as sc_:
                            LCc = alloc(sc_, "LCc", [128, 3, 16])
                            k.dma('sp', LCc[:], LCc_d, writes=['LCc'])
                            L1R, L1I = lam_l1(sc_, "pc", LCc[:, 0, :], LCc[:, 1, :], LCc[:, 2, :], 16, ['LCc'])
                            T1 = alloc(sc_, "pc_t1", [128, 16]); T2 = alloc(sc_, "pc_t2", [128, 16])
                            k.op(PE_, lambda e: e.tensor_copy(out=LAM[:, 1, 0, :], in_=L1R[:]), reads=[('pc', 'l1r')], writes=[('LAM', 1)])
                            k.op(PE_, lambda e: e.tensor_copy(out=LAM[:, 1, 1, :], in_=L1I[:]), reads=[('pc', 'l1i'), ('LAM', 1)], writes=[('LAM', 1)])
                            for j in range(2, 9):
                                cmul(LAM[:, j, 0, :], LAM[:, j, 1, :], LAM[:, j - 1, 0, :], LAM[:, j - 1, 1, :], LAM[:, 1, 0, :], LAM[:, 1, 1, :],
                                     (T1[:], T2[:]), [('LAM', j - 1), ('LAM', 1)], [('LAM', j), ('LAM', j)], 'pc')
                            k.op(PE_, lambda e: e.tensor_copy(out=LR2[:, :, 0], in_=LAM[:, 8, 0, :]), reads=[('LAM', 8)], writes=['LR2'])
                            k.op(PE_, lambda e: e.tensor_copy(out=LR2[:, :, 1], in_=LAM[:, 8, 0, :]), reads=[('LAM', 8), 'LR2'], writes=['LR2'])
                            ts(NLI[:], LAM[:, 8, 1, :], -1.0, None, ALU.mult, None, [('LAM', 8)], ['NLI'])
                            k.op(PE_, lambda e: e.tensor_copy(out=LAMB[:, 1, :, :], in_=LAM[:, 8, :, :]), reads=[('LAM', 8)], writes=['LAMB'])
                            for j in range(2, 17):
                                cmul(LAMB[:, j, 0, :], LAMB[:, j, 1, :], LAMB[:, j - 1, 0, :], LAMB[:, j - 1, 1, :], LAMB[:, 1, 0, :], LAMB[:, 1, 1, :],
                                     (T1[:], T2[:]), ['LAMB'], ['LAMB', 'LAMB'], 'pc')
                            ts(NLB[:], LAMB[:, :, 1, :], -1.0, None, ALU.mult, None, ['LAMB'], ['LAMB'])
                            k.op(PE_, lambda e: e.tensor_copy(out=L16[:, :, 0], in_=LAMB[:, 16, 0, :]), reads=['LAMB'], writes=['LAMB'])
                            k.op(PE_, lambda e: e.tensor_copy(out=L16[:, :, 1], in_=LAMB[:, 16, 0, :]), reads=['LAMB'], writes=['LAMB'])
                            _fence(k)
                        if stop == 'prepC':
                            return
                        with ExitStack() as sab:
                            with ExitStack() as sw5:
                                W5 = alloc(sw5, "W5", [128, 8, 768], BF16)
                                k.dma('pool', W5[:], w5_d.rearrange("(kc p) f -> p kc f", p=128), writes=['W5'])
                                for s in range(NSL):
                                    for q in range(6):
                                        b = 6 + (q % 2)
                                        for kc in range(8):
                                            MM(PB[b][:], lhsT=W5[:, kc, q * 128:(q + 1) * 128], rhs=XN[:, kc, sl(s)],
                                                                          start=(kc == 0), stop=(kc == 7), reads=['W5', ('XN', kc, s)], writes=[pbr(b)])
                                        k.op('act', lambda e: e.copy(out=U5[:, q, sl(s)], in_=PB[b][:]), reads=[pbr(b)], writes=[('U5', q, s)])
                                _fence(k)
                            if stop == 'proj':
                                return
                            E = alloc(sab, "E", [128, 16, 2, 256])
                            XS = [alloc(sab, "XS%d" % i, [128, 16, 2]) for i in range(2)]
                            AT = alloc(sab, "AT", [128, 16, 2]); BTt = alloc(sab, "BTt", [128, 16, 2])
                            RX = alloc(sab, "RX", [128, 32])
                            for s in range(NSL if stop != 'E1' else 1):
                                for q in range(6 if stop != 'E1' else 1):
                                    nb = NBQ[q]
                                    for part in range(2):
                                        for j in range(8):
                                            for b4 in range(nb):
                                                b = 3 * (q % 2) + b4
                                                c0 = part * 64
                                                MM(
                                                    PB[b][:, c0:c0 + 64],
                                                    lhsT=WIN[32 * b4:32 * b4 + 32, j, part, q * 128:(q + 1) * 128],
                                                    rhs=U5[32 * b4:32 * b4 + 32, q, sl(s)].rearrange("p (c s) -> p c s", s=8)[:, :, 7 - j],
                                                    start=(j == 0), stop=(j == 7), reads=[('WIN', j, part), ('U5', q, s)], writes=[pbr(b)])
                                    for b4 in range(nb):
                                        b = 3 * (q % 2) + b4
                                        for part in range(2):
                                            k.op('act' if part == 0 else 'dve',
                                                 (lambda e: e.copy(out=E[:, 3 * q + b4, part, 64 * s:64 * s + 64], in_=PB[b][:, part * 64:(part + 1) * 64])) if part == 0 else
                                                 (lambda e: e.tensor_copy(out=E[:, 3 * q + b4, part, 64 * s:64 * s + 64], in_=PB[b][:, part * 64:(part + 1) * 64])),
                                                 reads=[pbr(b)], writes=[('E', q, s, b4)])
                            er = [('E', q, s, b4) for q in range(6) for s in range(NSL) for b4 in range(3)]
                            if stop in ('E', 'E1'):
                                _fence(k)
                                return

                            E5 = E[:].rearrange("p a r (sb i) -> p a r sb i", i=16)
                            A4 = alloc(sab, "A4", [128, 16, 2, 16]); B4 = alloc(sab, "B4", [128, 16, 2, 16])
                            XSB = alloc(sab, "XSB", [128, 16, 2, 17])
                            bc3 = lambda a2: a2.unsqueeze(2).unsqueeze(3).broadcast_to([128, 16, 2, 16])
                            bc2 = lambda a2: a2.unsqueeze(2).broadcast_to([128, 16, 16])

                            def cstep(src, j, add_to, rr, wres):
                                k.op('dve', lambda e: e.tensor_tensor(out=A4[:], in0=src, in1=bc3(LAMB[:, j, 0, :]), op=ALU.mult),
                                     reads=rr + ['LAMB'], writes=['A4'])
                                k.op('dve', lambda e: e.tensor_tensor(out=B4[:, :, 0, :], in0=src[:, :, 1, :], in1=bc2(NLB[:, j, :]), op=ALU.mult),
                                     reads=rr + ['LAMB'], writes=['B40'])
                                k.op('dve', lambda e: e.tensor_tensor(out=B4[:, :, 1, :], in0=src[:, :, 0, :], in1=bc2(LAMB[:, j, 1, :]), op=ALU.mult),
                                     reads=rr + ['LAMB'], writes=['B41'])
                                k.op('dve', lambda e: e.tensor_tensor(out=A4[:], in0=A4[:], in1=B4[:], op=ALU.add),
                                     reads=['A4', 'B40', 'B41'], writes=['A4'])
                                k.op('dve', lambda e: e.tensor_tensor(out=add_to, in0=A4[:], in1=add_to, op=ALU.add),
                                     reads=['A4'] + rr, writes=wres)

                            for i in range(1, 16):
                                cstep(E5[:, :, :, :, i - 1], 1, E5[:, :, :, :, i], er if i == 1 else ['E5'], ['E5'])

                            def level1b(init_ap, init_res):
                                k.op('dve', lambda e: e.tensor_copy(out=XSB[:, :, :, 0], in_=init_ap), reads=init_res + ['XSB'], writes=['XSB'])
                                for sb_ in range(16):
                                    k.op('dve', lambda e: e.tensor_tensor(out=AT[:], in0=XSB[:, :, :, sb_], in1=L16[:], op=ALU.mult),
                                         reads=['XSB', 'LAMB'], writes=['AT'])
                                    k.op('dve', lambda e: e.tensor_tensor(out=BTt[:, :, 0], in0=XSB[:, :, 1, sb_], in1=NLB[:, 16, :], op=ALU.mult),
                                         reads=['XSB', 'LAMB'], writes=['BT0'])
                                    k.op('dve', lambda e: e.tensor_tensor(out=BTt[:, :, 1], in0=XSB[:, :, 0, sb_], in1=LAMB[:, 16, 1, :], op=ALU.mult),
                                         reads=['XSB', 'LAMB'], writes=['BT1'])
                                    k.op('dve', lambda e: e.tensor_tensor(out=AT[:], in0=AT[:], in1=BTt[:], op=ALU.add),
                                         reads=['AT', 'BT0', 'BT1'], writes=['AT'])
                                    k.op('dve', lambda e: e.tensor_tensor(out=XSB[:, :, :, sb_ + 1], in0=AT[:], in1=E5[:, :, :, sb_, 15], op=ALU.add),
                                         reads=['AT', 'E5'], writes=['XSB'])

                            k.op('pool', lambda e: e.memset(XINIT[:], 0.0), writes=['XINIT'])
                            level1b(XINIT[:], ['XINIT'])
                            k.op('pool', lambda e: e.tensor_copy(out=XS[0][:], in_=XSB[:, :, :, 16]), reads=['XSB'], writes=[('XS', 0)])
                            if stop == 'passA':
                                _fence(k)
                                return
                            k.dma('sp', cin_s.ap(), XS[0][:].rearrange("p a b -> p (a b)"), reads=[('XS', 0)], writes=['cin_s'])
                            k.collective(lambda e: e.collective_compute("AllGather", ALU.bypass, replica_groups=PAIRS,
                                                                        ins=[cin_s.ap().opt()], outs=[cout_s.ap().opt()]),
                                         reads=['cin_s'], writes=['cout_s'])
                            k.dma('sp', RX[:], cout_s.ap()[0:128, :], reads=['cout_s'], writes=['RX'])
                            k.op('pool', lambda e: e.tensor_scalar(out=XINIT[:].rearrange("p a b -> p (a b)"), in0=RX[:], scalar1=FLAG[:, 0:1],
                                                                   scalar2=None, op0=ALU.mult),
                                 reads=['RX', 'FLAG', 'XINIT'], writes=['XINIT'])
                            if stop == 'coll':
                                _fence(k)
                                return
                            level1b(XINIT[:], ['XINIT'])
                            for i in range(16):
                                cstep(XSB[:, :, :, 0:16], i + 1, E5[:, :, :, :, i], ['XSB', 'E5'], ['E5'])
                            k.op('pool', lambda e: e.tensor_copy(out=AT[:], in_=AT[:]), reads=['E5', 'AT'], writes=[('Ec', 255), 'AT'])
                            k.op('pool', lambda e: e.tensor_copy(out=X1B[:, :, :, 0], in_=XINIT[:]), reads=['XINIT'], writes=['X1B0'])
                            for P_ in range(16):
                                k.op('dve', lambda e: e.tensor_copy(out=X1B[:, P_, :, 1:257], in_=E[:, P_, :, :]), reads=[('Ec', 255)], writes=['X1B'])
                            _fence(k)
                        if stop == 'passB':
                            return
                        with ExitStack() as ssl:
                            CCOR = alloc(ssl, "CCOR", [128, 8, 2, 512], BF16)
                            C0 = alloc(ssl, "C0", [128, 2, 512], BF16)
                            with ExitStack() as spc:
                                CC = alloc(spc, "CC", [128, 2, 512])
                                k.dma('sp', CC[:], CC_d, writes=['CC'])
                                Wt = [alloc(spc, "cw%d" % i, [128, 512]) for i in range(4)]
                                NLP = alloc(spc, "NLP", [128, 16])
                                v3 = lambda ap: ap.rearrange("p (a b) -> p a b", a=16)
                                k.op(PE_, lambda e: e.tensor_copy(out=C0[:, 0, :], in_=CC[:, 0, :]), reads=['CC'], writes=[('C0', 0)])
                                ts(C0[:, 1, :], CC[:, 1, :], -1.0, None, ALU.mult, None, ['CC'], [('C0', 1)])
                                for s8 in range(8):
                                    lr = LAM[:, s8 + 1, 0, :].unsqueeze(2).broadcast_to([128, 16, 32])
                                    li = LAM[:, s8 + 1, 1, :].unsqueeze(2).broadcast_to([128, 16, 32])
                                    ts(NLP[:], LAM[:, s8 + 1, 1, :], -1.0, None, ALU.mult, None, [('LAM', s8 + 1)], ['NLP'])
                                    nli = NLP[:].unsqueeze(2).broadcast_to([128, 16, 32])
                                    tt(v3(Wt[0][:]), v3(CC[:, 0, :]), lr, ALU.mult, ['CC', ('LAM', s8 + 1)], ['cw0'])
                                    tt(v3(Wt[1][:]), v3(CC[:, 1, :]), li, ALU.mult, ['CC', ('LAM', s8 + 1)], ['cw1'])
                                    tt(CCOR[:, s8, 0, :], Wt[0][:], Wt[1][:], ALU.subtract, ['cw0', 'cw1'], [('CCOR', s8, 0)])
                                    tt(v3(Wt[2][:]), v3(CC[:, 0, :]), nli, ALU.mult, ['CC', 'NLP'], ['cw2'])
                                    tt(v3(Wt[3][:]), v3(CC[:, 1, :]), lr, ALU.mult, ['CC', ('LAM', s8 + 1)], ['cw3'])
                                    tt(CCOR[:, s8, 1, :], Wt[2][:], Wt[3][:], ALU.subtract, ['cw2', 'cw3'], [('CCOR', s8, 1)])
                                _fence(k)
                            ZB = [alloc(ssl, "ZB%d" % i, [128, 512], BF16) for i in range(12)]
                            YF = [alloc(ssl, "YF%d" % i, [128, 512]) for i in range(2)]
                            GT = [alloc(ssl, "GT%d" % i, [128, 512]) for i in range(2)]
                            v8 = lambda ap: ap.rearrange("p (c s) -> p c s", s=8)

                            def phase1(s, q):
                                nb = NBQ[q]
                                zo = 6 * (q % 2)
                                for j in range(8):
                                    for b4 in range(nb):
                                        band = slice(32 * b4, 32 * b4 + 32)
                                        for part in range(2):
                                            zb = (0, 2, 6)[b4] + part
                                            MM(
                                                v8(PB[zb][:])[:, :, j:8],
                                                lhsT=WIN[band, j, part, q * 128:(q + 1) * 128],
                                                rhs=v8(U5[band, q, sl(s)])[:, :, 0:8 - j],
                                                start=(j == 0), stop=(j == 7), reads=[('WIN', j, part), ('U5', q, s)], writes=[pbr(zb)])
                                for b4 in range(nb):
                                    for part in range(2):
                                        zb = (0, 2, 6)[b4] + part
                                        zt = zo + 2 * b4 + part
                                        if part == 0:
                                            k.op('act', lambda e: e.copy(out=ZB[zt][:], in_=PB[zb][:]), reads=[pbr(zb)], writes=[('ZB', zt)])
                                        else:
                                            k.op('dve', lambda e: e.tensor_copy(out=ZB[zt][:], in_=PB[zb][:]), reads=[pbr(zb)], writes=[('ZB', zt)])

                            def phase2(s, q):
                                yb = 4 + q % 2
                                nb = NBQ[q]
                                zo = 6 * (q % 2)
                                pr_ = slice(0, 32 * nb)
                                for b4 in range(nb):
                                    P = 3 * q + b4
                                    band = slice(32 * b4, 32 * b4 + 32)
                                    for part in range(2):
                                        zt = zo + 2 * b4 + part
                                        MM(PB[yb][band, :], lhsT=C0[:, part, 32 * P:32 * P + 32], rhs=ZB[zt][:],
                                                                      start=(part == 0), stop=False, reads=[('C0', part), ('ZB', zt)], writes=[pbr(yb)])
                                    for s8 in range(8):
                                        for part in range(2):
                                            MM(v8(PB[yb][band, :])[:, :, s8],
                                                                          lhsT=CCOR[:, s8, part, 32 * P:32 * P + 32],
                                                                          rhs=X1B[:, P, part, 64 * s:64 * s + 64],
                                                                          start=False, stop=(s8 == 7 and part == 1), reads=[('CCOR', s8, part), 'X1B', 'X1B0'], writes=[pbr(yb)])
                                f = q % 2
                                k.op('dve', lambda e: e.scalar_tensor_tensor(out=YF[f][pr_, :], in0=U5[pr_, q, sl(s)], scalar=D5[pr_, q:q + 1],
                                                                             in1=PB[yb][pr_, :], op0=ALU.mult, op1=ALU.add),
                                     reads=[('U5', q, s), 'D5', pbr(yb)], writes=[('YF', f)])
                                k.op('dve', lambda e: e.tensor_tensor(out=GT[f][pr_, :], in0=YF[f][pr_, :], in1=YF[f][pr_, :], op=ALU.mult),
                                     reads=[('YF', f)], writes=[('GT', f)])
                                k.op('dve', lambda e: e.tensor_scalar(out=GT[f][pr_, :], in0=GT[f][pr_, :], scalar1=0.044715, scalar2=1.0,
                                                                       op0=ALU.mult, op1=ALU.add),
                                     reads=[('GT', f)], writes=[('GT', f)])
                                k.op('dve', lambda e: e.tensor_tensor(out=GT[f][pr_, :], in0=GT[f][pr_, :], in1=YF[f][pr_, :], op=ALU.mult),
                                     reads=[('GT', f), ('YF', f)], writes=[('GT', f)])
                                k.op('act', lambda e: e.activation(out=GT[f][pr_, :], in_=GT[f][pr_, :], func=AF.Sigmoid, scale=GELU_K),
                                     reads=[('GT', f)], writes=[('GT', f)])
                                k.op('dve', lambda e: e.tensor_tensor(out=U5[pr_, q, sl(s)], in0=GT[f][pr_, :], in1=YF[f][pr_, :], op=ALU.mult),
                                     reads=[('GT', f), ('YF', f), ('U5', q, s)], writes=[('U5', q, s)])

                            items = [(s, q) for s in range(NSL) for q in range(6)]
                            for idx, (s, q) in enumerate(items):
                                phase1(s, q)
                                if idx > 0:
                                    phase2(*items[idx - 1])
                            phase2(*items[-1])
                            _fence(k)
                    if stop == 'slab':
                        return
                    with ExitStack() as stl:
                        tail(stl, 6, lambda kc, s: U5[:, kc, sl(s)], lambda kc, s: ('U5', kc, s), glv, glg, W_GA, after_slab=spill_slab)

            def hg_core(OG):
                with ExitStack() as sh:
                        CONST = alloc(sh, "CONST", [128, 768])
                        MASKC = CONST[:, 512:640]
                        IDb = alloc(sh, "IDb", [128, 128], BF16)
                        LBR = alloc(sh, "LBR", [128, 2, 8]); LB = alloc(sh, "LB", [128, 8]); OML = alloc(sh, "OML", [128, 8])
                        NOML = alloc(sh, "NOML", [128, 8]); GN = alloc(sh, "GN", [128, 8])
                        k.dma('sp', CONST[:], consts, writes=['CONST'])
                        k.dma('sp', LBR[:], hglb_d, writes=['LBR'])
                        k.dma('sp', GN[:], hggn_d, writes=['GN'])
                        k.op('dve', lambda e: e.tensor_copy(out=IDb[:], in_=CONST[:, 640:768]), reads=['CONST'], writes=['IDb'])
                        k.op('dve', lambda e: e.tensor_tensor(out=LB[:], in0=LBR[:, 0, :], in1=LBR[:, 1, :], op=ALU.subtract),
                             reads=['LBR'], writes=['LB'])
                        k.op('act', lambda e: e.activation(out=LB[:], in_=LB[:], func=AF.Sigmoid), reads=['LB'], writes=['LB'])
                        k.op('dve', lambda e: e.tensor_scalar(out=OML[:], in0=LB[:], scalar1=-1.0, scalar2=1.0, op0=ALU.mult, op1=ALU.add),
                             reads=['LB'], writes=['OML'])
                        k.op('dve', lambda e: e.tensor_scalar(out=NOML[:], in0=LB[:], scalar1=-1.0, scalar2=None, op0=ALU.add),
                             reads=['LB'], writes=['NOML'])
                        NB_ = 2
                        WQ2, WF2, WI2, WGt2 = [[alloc(sh, n_ + str(i), [128, 8, 128], BF16) for i in range(2)] for n_ in ("WQ", "WF", "WI", "WGt")]
                        QTs = [alloc(sh, "QT%d" % i, [128, T], BF16) for i in range(NB_)]
                        KTs = [alloc(sh, "KT%d" % i, [128, T], BF16) for i in range(NB_)]
                        KTTs = [alloc(sh, "KTT%d" % i, [128, 16, 128], BF16) for i in range(NB_)]
                        VTs = [alloc(sh, "VT%d" % i, [128, 16, 128], BF16) for i in range(NB_)]
                        SGts = [alloc(sh, "SGt%d" % i, [128, T], BF16) for i in range(NB_)]
                        DKs = [alloc(sh, "DK%d" % i, [128, 32]) for i in range(NB_)]
                        SINITs = [alloc(sh, "SINIT%d" % i, [128, 128]) for i in range(NB_)]
                        SFINs = [alloc(sh, "SFIN%d" % i, [128, 128]) for i in range(NB_)]
                        RXHs = [alloc(sh, "RXH%d" % i, [128, 128]) for i in range(NB_)]
                        SGM2, LF2, GC2, EG2, ENG2, KK2, SGG2 = [[alloc(sh, n_ + str(i), [128, 512]) for i in range(2)]
                                                                for n_ in ("SGM", "LF", "GC", "EG", "ENG", "KK", "SGG")]
                        TPa = [alloc(sh, "TPa%d" % i, [128, 128]) for i in range(2)]
                        TPb = [alloc(sh, "TPb%d" % i, [128, 128]) for i in range(2)]
                        SALL = alloc(sh, "SALL", [128, 32, 128], BF16)
                        SC = [alloc(sh, "SC%d" % i, [128, 128], BF16) for i in range(2)]
                        OSQ = alloc(sh, "OSQ", [128, 512], BF16); ORS = alloc(sh, "ORS", [128, 512]); OT = alloc(sh, "OT", [128, 512])
                        PT7 = PB[7][:].bitcast(BF16)
                        SCALE = float(128 ** -0.5)

                        def load_w(h_):
                            wb_ = h_ % 2
                            for (Wl, c0, nm_) in ((WQ2, W_Q, 'WQ'), (WF2, W_F, 'WF'), (WI2, W_I, 'WI'), (WGt2, W_G, 'WGt')):
                                k.dma('pool', Wl[wb_][:], w_in[:, c0 + h_ * 128:c0 + (h_ + 1) * 128].rearrange("(kc p) f -> p kc f", p=128),
                                      writes=[(nm_, wb_)])

                        def kv(hb, c, which):
                            t_i, hh = c // 2, c % 2
                            pb = 4 + hh
                            co = 0
                            MM(PB[pb][:, co:co + 128], lhsT=KTTs[hb][64 * hh:64 * hh + 64, t_i, :],
                                                          rhs=VTs[hb][64 * hh:64 * hh + 64, t_i, :], start=True, stop=True, reads=[('KTT', hb, t_i // 4), ('VT', hb, t_i // 4)], writes=[('PBh', pb, which)])
                            return pb

                        def chain(hb, init_ap, init_res, store):
                            DK = DKs[hb]
                            TP = TPb if store else TPa
                            tn = 'TPb' if store else 'TPa'
                            for c in range(32):
                                if c > 0:
                                    yield
                                wh = 0
                                pb = kv(hb, c, wh)
                                dst, src = TP[c % 2], TP[(c + 1) % 2]
                                if c == 0:
                                    if init_ap is None:
                                        k.op('dve', lambda e: e.tensor_copy(out=dst[:], in_=PB[pb][:, wh * 128:wh * 128 + 128]), reads=[('PBh', pb, wh)], writes=[(tn, 0)])
                                    else:
                                        k.op('dve', lambda e: e.tensor_tensor(out=dst[:], in0=init_ap, in1=PB[pb][:, wh * 128:wh * 128 + 128], op=ALU.add),
                                             reads=init_res + [('PBh', pb, wh)], writes=[(tn, 0)])
                                        k.op('act', lambda e: e.copy(out=SALL[:, 0, :], in_=init_ap), reads=init_res, writes=[('SALL', 0)])
                                else:
                                    if store:
                                        k.op('act', lambda e: e.activation(out=SALL[:, c, :], in_=src[:], func=AF.Copy, scale=DK[:, c - 1:c]),
                                             reads=[(tn, (c + 1) % 2), ('DK', hb, (c - 1) // 8)], writes=[('SALL', c)])
                                    k.op('dve', lambda e: e.scalar_tensor_tensor(out=dst[:], in0=src[:], scalar=DK[:, c - 1:c], in1=PB[pb][:, wh * 128:wh * 128 + 128],
                                                                                 op0=ALU.mult, op1=ALU.add),
                                         reads=[(tn, (c + 1) % 2), ('DK', hb, (c - 1) // 8), ('PBh', pb, wh)], writes=[(tn, c % 2)])

                        def slab_a1(hd, s):
                            hb = hd % NB_
                            wb = hd % 2
                            WQ, WF, WI, WGt = WQ2[wb], WF2[wb], WI2[wb], WGt2[wb]
                            QT, KT, VT, SGt, DK = QTs[hb], KTs[hb], VTs[hb], SGts[hb], DKs[hb]
                            tb = s % 2
                            SGM, LF, KK, GC, EG, ENG, SGG = SGM2[tb], LF2[tb], KK2[tb], GC2[tb], EG2[tb], ENG2[tb], SGG2[tb]
                            qb = 1 if s % 2 == 0 else 6
                            for kc in range(8):
                                MM(PB[0][:], lhsT=WF[:, kc, :], rhs=XN[:, kc, sl(s)], start=(kc == 0), stop=(kc == 7), reads=[('WF', wb), ('XN', kc, s)], writes=[pbr(0)])
                            k.op('act', lambda e: e.activation(out=SGM[:], in_=PB[0][:], func=AF.Sigmoid), reads=[pbr(0)], writes=[('SGM', tb)])
                            for kc in range(8):
                                MM(PB[2][:], lhsT=WGt[:, kc, :], rhs=XN[:, kc, sl(s)], start=(kc == 0), stop=(kc == 7), reads=[('WGt', wb), ('XN', kc, s)], writes=[pbr(2)])
                            k.op('act', lambda e: e.activation(out=SGG[:], in_=PB[2][:], func=AF.Sigmoid), reads=[pbr(2)], writes=[('SGG', tb)])
                            k.op('dve', lambda e: e.tensor_tensor(out=SGt[:, sl(s)], in0=SGG[:], in1=PB[2][:], op=ALU.mult),
                                 reads=[('SGG', tb), pbr(2)], writes=[('SGt', hb, s)])
                            k.op('act', lambda e: e.activation(out=LF[:], in_=SGM[:], func=AF.Ln, scale=OML[:, hd:hd + 1], bias=LB[:, hd:hd + 1]),
                                 reads=[('SGM', tb), 'OML', 'LB'], writes=[('LF', tb)])
                            k.op('act', lambda e: e.activation(out=KK[:], in_=SGM[:], func=AF.Identity, scale=NOML[:, hd:hd + 1], bias=OML[:, hd:hd + 1]),
                                 reads=[('SGM', tb), 'NOML', 'OML'], writes=[('KK', tb)])
                            k.op('dve', lambda e: e.tensor_tensor_scan(out=GC[:], data0=CONST[:, 0:512], data1=LF[:], initial=0.0,
                                                                       op0=ALU.mult, op1=ALU.add),
                                 reads=['CONST', ('LF', tb)], writes=[('GC', tb)])
                            k.op('act', lambda e: e.activation(out=EG[:], in_=GC[:], func=AF.Exp), reads=[('GC', tb)], writes=[('EG', tb)])
                            k.op('act', lambda e: e.activation(out=ENG[:], in_=GC[:], func=AF.Exp, scale=-1.0), reads=[('GC', tb)], writes=[('ENG', tb)])
                            k.op('dve', lambda e: e.tensor_copy(out=DK[:, 8 * s:8 * s + 8],
                                                                in_=EG[:].rearrange("p (c s) -> p c s", s=64)[:, :, 63]),
                                 reads=[('EG', tb)], writes=[('DK', hb, s)])
                            for kc in range(8):
                                MM(PB[qb][:], lhsT=WQ[:, kc, :], rhs=XN[:, kc, sl(s)], start=(kc == 0), stop=(kc == 7), reads=[('WQ', wb), ('XN', kc, s)], writes=[pbr(qb)])
                            k.op('dve', lambda e: e.scalar_tensor_tensor(out=QT[:, sl(s)], in0=PB[qb][:], scalar=SCALE, in1=EG[:],
                                                                         op0=ALU.mult, op1=ALU.mult),
                                 reads=[pbr(qb), ('EG', tb)], writes=[('QT', hb, s)])
                            k.op('pool', lambda e: e.tensor_tensor(out=KT[:, sl(s)], in0=KK[:], in1=ENG[:], op=ALU.mult),
                                 reads=[('KK', tb), ('ENG', tb)], writes=[('KT', hb, s)])
                            for t4 in range(4):
                                tk = slice(s * 512 + t4 * 128, s * 512 + (t4 + 1) * 128)
                                for kc in range(8):
                                    MM(PB[3][:, t4 * 128:(t4 + 1) * 128], lhsT=XN[:, kc, tk], rhs=WI[:, kc, :],
                                                                  start=(kc == 0), stop=(kc == 7), reads=[('WI', wb), ('XN', kc, s)], writes=[pbr(3)])
                            k.op('act', lambda e: e.copy(out=VT[:, 4 * s:4 * s + 4, :].rearrange("p a b -> p (a b)"), in_=PB[3][:]),
                                 reads=[pbr(3)], writes=[('VT', hb, s)])

                        def slab_a2(hd, s):
                            hb = hd % NB_
                            KT, KTT = KTs[hb], KTTs[hb]
                            for t4 in range(4):
                                k.op('pe', lambda e: e.transpose(PT7[:, t4 * 128:(t4 + 1) * 128], KT[:, s * 512 + t4 * 128:s * 512 + (t4 + 1) * 128], IDb[:]),
                                     reads=[('KT', hb, s), 'IDb'], writes=[pbr(7)])
                            k.op('act', lambda e: e.copy(out=KTT[:, 4 * s:4 * s + 4, :].rearrange("p a b -> p (a b)"), in_=PT7[:, 0:512]),
                                 reads=[pbr(7)], writes=[('KTT', hb, s)])

                        def pump(g, n):
                            if g is None:
                                return
                            for _ in range(n):
                                try:
                                    next(g)
                                except StopIteration:
                                    return

                        def stage_a(hd, gb):
                            if os.environ.get('NOINTER'):
                                pump(gb, 64)
                            for s in range(NSL):
                                slab_a1(hd, s)
                                if s > 0:
                                    slab_a2(hd, s - 1)
                                pump(gb, (0, 10, 11, 12)[s])
                            slab_a2(hd, NSL - 1)
                            pump(gb, 64)
                            if hd + 1 < 8:
                                load_w(hd + 1)

                        def finish_a(hd):
                            hb = hd % NB_
                            k.op('dve', lambda e: e.tensor_scalar(out=SFINs[hb][:], in0=TPa[1][:], scalar1=DKs[hb][:, 31:32], scalar2=None, op0=ALU.mult),
                                 reads=[('TPa', 1), ('DK', hb, 3)], writes=[('SFIN', hb)])
                            k.dma('sp', cin_h[hd].ap(), SFINs[hb][:], reads=[('SFIN', hb)], writes=[('cin_h', hd)])
                            k.collective(lambda e: e.collective_compute("AllGather", ALU.bypass, replica_groups=PAIRS,
                                                                        ins=[cin_h[hd].ap().opt()], outs=[cout_h[hd].ap().opt()]),
                                         reads=[('cin_h', hd)], writes=[('cout_h', hd)])
                            k.dma('sp', RXHs[hb][:], cout_h[hd].ap()[0:128, :], reads=[('cout_h', hd)], writes=[('RXH', hb)])
                            k.op('pool', lambda e: e.tensor_scalar(out=SINITs[hb][:], in0=RXHs[hb][:], scalar1=FLAG[:, 0:1], scalar2=None, op0=ALU.mult),
                                 reads=[('RXH', hb), 'FLAG'], writes=[('SINIT', hb)])

                        def stage_b(hd, ga):
                            hb = hd % NB_
                            QT, KT, VT, SGt = QTs[hb], KTs[hb], VTs[hb], SGts[hb]
                            if os.environ.get('NOINTER'):
                                pump(ga, 64)
                            for t_i in range(16):
                                pump(ga, 2)
                                s = t_i // 4
                                ob = 2 + s % 2
                                tk = slice(t_i * 128, (t_i + 1) * 128)
                                oc = slice((t_i % 4) * 128, (t_i % 4 + 1) * 128)
                                sb_ = t_i % 2
                                MM(PB[sb_][:, 0:128], lhsT=KT[:, tk], rhs=QT[:, tk], start=True, stop=True, reads=[('KT', hb, s), ('QT', hb, s)], writes=[pbr(sb_)])
                                k.op('dve', lambda e: e.tensor_tensor(out=SC[sb_][:], in0=PB[sb_][:, 0:128], in1=MASKC, op=ALU.mult),
                                     reads=[pbr(sb_), 'CONST'], writes=[('SC', sb_)])
                                MM(PB[ob][:, oc], lhsT=VT[:, t_i, :], rhs=SC[sb_][:], start=True, stop=False, reads=[('VT', hb, s), ('SC', sb_)], writes=[pbr(ob)])
                                for hh in range(2):
                                    c = 2 * t_i + hh
                                    MM(PB[ob][:, t_i % 4 * 128 + 64 * hh:t_i % 4 * 128 + 64 * hh + 64], lhsT=SALL[:, c, :],
                                                                  rhs=QT[:, t_i * 128 + 64 * hh:t_i * 128 + 64 * hh + 64], start=False, stop=(hh == 1), reads=[('SALL', c), ('QT', hb, s)], writes=[pbr(ob)])
                                if t_i % 4 == 3:
                                    k.op('act', lambda e: e.activation(out=OSQ[:], in_=PB[ob][:], func=AF.Square), reads=[pbr(ob)], writes=['OSQ'])
                                    MM(PB[6][:], lhsT=ONES[:], rhs=OSQ[:], start=True, stop=True, reads=['ONES', 'OSQ'], writes=[pbr(6)])
                                    k.op('act', lambda e: e.activation(out=ORS[:], in_=PB[6][:], func=AF.Sqrt, bias=EPS, scale=1.0 / 128),
                                         reads=[pbr(6)], writes=['ORS'])
                                    k.op('dve', lambda e: e.reciprocal(ORS[:], ORS[:]), reads=['ORS'], writes=['ORS'])
                                    k.op('dve', lambda e: e.scalar_tensor_tensor(out=OT[:], in0=PB[ob][:], scalar=GN[:, hd:hd + 1], in1=ORS[:],
                                                                                 op0=ALU.mult, op1=ALU.mult),
                                         reads=[pbr(ob), 'GN', 'ORS'], writes=['OT'])
                                    k.op('pool', lambda e: e.tensor_tensor(out=OG[:, hd, sl(s)], in0=OT[:], in1=SGt[:, sl(s)], op=ALU.mult),
                                         reads=['OT', ('SGt', hb, s)], writes=[('OG', hd, s)])

                        load_w(0)
                        stage_a(0, None)
                        ga = chain(0, None, [], False)
                        next(ga)
                        pump(ga, 64)
                        finish_a(0)
                        for hd in range(8):
                            hb = hd % NB_
                            gb = chain(hb, SINITs[hb][:], [('SINIT', hb)], True)
                            if hd + 1 < 8:
                                stage_a(hd + 1, gb)
                                ga = chain((hd + 1) % NB_, None, [], False)
                                next(ga)
                            else:
                                pump(gb, 64)
                                ga = None
                            stage_b(hd, ga)
                            if hd + 1 < 8:
                                pump(ga, 64)
                                finish_a(hd + 1)
                        _fence(k)

            def ple():
                with ExitStack() as st:
                    norm_to_xn(st, 3)
                    WPG = alloc(st, "WPG", [128, 8, D], BF16); WPP = alloc(st, "WPP", [128, 2, D], BF16)
                    PTb = alloc(st, "PTb", [128, 2, T], BF16)
                    S1 = [alloc(st, "PS1%d" % i, [128, 512]) for i in range(2)]
                    TT = [alloc(st, "PTT%d" % i, [128, 512]) for i in range(2)]
                    k.dma('pool', WPG[:], wpg.rearrange("(kc p) f -> p kc f", p=128), writes=['WPG'])
                    k.dma('pool', WPP[:], wpp.rearrange("(kc p) f -> p kc f", p=128), writes=['WPP'])
                    for s_ in range(NSL):
                        k.dma('pool', PTb[:, :, sl(s_)], pT[:, sl(s_)].rearrange("(kc p) t -> p kc t", p=128), writes=['PTb'])
                    for s in range(NSL):
                        for dc in range(8):
                            b = dc % 2
                            cs = slice(dc * 128, (dc + 1) * 128)
                            for kc in range(8):
                                MM(PB[b][:], lhsT=WPG[:, kc, cs], rhs=XN[:, kc, sl(s)], start=(kc == 0), stop=(kc == 7), reads=['WPG', ('XN', kc, s)], writes=[pbr(b)])
                            for kc in range(2):
                                MM(PB[2 + b][:], lhsT=WPP[:, kc, cs], rhs=PTb[:, kc, sl(s)], start=(kc == 0), stop=(kc == 1), reads=['WPP', 'PTb'], writes=[pbr(2 + b)])
                            k.op('act', lambda e: e.activation(out=S1[b][:], in_=PB[b][:], func=AF.Sigmoid), reads=[pbr(b)], writes=[('PS1', b)])
                            k.op('dve', lambda e: e.tensor_tensor(out=TT[b][:], in0=S1[b][:], in1=PB[2 + b][:], op=ALU.mult),
                                 reads=[('PS1', b), pbr(2 + b)], writes=[('PTT', b)])
                            k.op('dve', lambda e: e.tensor_tensor(out=Hh[0][:, dc, sl(s)], in0=TT[b][:], in1=Hh[0][:, dc, sl(s)], op=ALU.add),
                                 reads=[('PTT', b), ('H', dc, s)], writes=[('H', dc, s)])
                    _fence(k)

            if 'ffn1' in stages:
                ffn(0, w1g, w1u, w1d)
            if 's5' in stages or 'hg' in stages:
                with ExitStack() as st:
                    norm_to_xn(st, 1)
                    _fence(k)
                if 's5' in stages:
                    s5_branch()
                if 'hg' in stages:
                    hres = [('H', kc, s_) for kc in range(8) for s_ in range(NSL)]
                    if 's5' not in stages:
                        for s_ in range(NSL):
                            spill_slab(s_)
                    _fence(k)
                    hst[0].close()
                    sB = ExitStack()
                    sBh.append(sB)
                    OG = alloc(sB, "OG", [128, 8, T], BF16)
                    hg_core(OG)
                    hst[0] = ExitStack()
                    Hh[0] = alloc(hst[0], "H", [128, 8, T])
                    for kc in range(8):
                        k.dma('sp', Hh[0][:, kc, :], hsp.ap()[:, kc, :], reads=[('hsp', kc, s_) for s_ in range(NSL)], writes=[('H', kc, s_) for s_ in range(NSL)])
                    with ExitStack() as stl:
                        tail(stl, 8, lambda kc, s: OG[:, kc, sl(s)], lambda kc, s: ('OG', kc, s), hgwo, None, W_GB)
            if 'ffn2' in stages:
                ffn(2, w2g, w2u, w2d)
            if 'ple' in stages:
                ple()

            with ExitStack() as st:
                if final_norm:
                    def emit(s, kc, RS):
                        k.op('dve', lambda e: e.scalar_tensor_tensor(out=Hh[0][:, kc, sl(s)], in0=Hh[0][:, kc, sl(s)],
                                                                     scalar=G[:, 4, kc:kc + 1], in1=RS[:],
                                                                     op0=ALU.mult, op1=ALU.mult),
                             reads=[('H', kc, s), 'RS', 'G'], writes=[('H', kc, s)])
                        if kc == 7:
                            for c2 in range(8):
                                k.dma('sp', outT[c2 * 128:(c2 + 1) * 128, sl(s)], Hh[0][:, c2, sl(s)],
                                      reads=[('H', c2, s)], writes=[('OUT', c2, s)])
                    rmsnorm(st, emit)
                else:
                    for s in range(NSL):
                        for c2 in range(8):
                            k.dma('sp', outT[c2 * 128:(c2 + 1) * 128, sl(s)], Hh[0][:, c2, sl(s)],
                                  reads=[('H', c2, s)], writes=[('OUT', c2, s)])
                k.finish('sp', [('OUT', c2, s) for c2 in range(8) for s in range(NSL)])
            hst[0].close()
            for sb_ in sBh:
                sb_.close()
    return nc


def _s5_layouts(inputs):
    lre = np.asarray(inputs['s5_lam_re'][0], np.float32); lim = np.asarray(inputs['s5_lam_im'][0], np.float32)
    ldt = np.asarray(inputs['s5_log_dt'][0], np.float32)
    bre = np.asarray(inputs['s5_b_re'][0], np.float32); bim = np.asarray(inputs['s5_b_im'][0], np.float32)
    cre = np.asarray(inputs['s5_c_re'][0], np.float32); cim = np.asarray(inputs['s5_c_im'][0], np.float32)
    w_in = np.asarray(inputs['w_in'][0], np.float32)
    gv = np.asarray(inputs['s5_glu_val'][0], np.float32); gg = np.asarray(inputs['s5_glu_gate'][0], np.float32)
    d5 = np.asarray(inputs['s5_d'][0], np.float32)
    LT = np.zeros((128, 3, 6, 128), np.float32); BT = np.zeros((128, 2, 6, 128), np.float32)
    W5 = np.zeros((D, 6, 128), np.float32); GV = np.zeros((6, 128, D), np.float32); GG = np.zeros((6, 128, D), np.float32)
    D5 = np.zeros((128, 6), np.float32)
    for q in range(6):
        for b in range(3):
            P = min(3 * q + b, 15)
            valid = (3 * q + b) < 16
            for g2p in range(2):
                g = 2 * P + g2p
                cs = slice(g2p * 64, (g2p + 1) * 64)
                LT[:, 0, q, cs][32 * b:32 * b + 32] = lre[g][None, :]
                LT[:, 1, q, cs][32 * b:32 * b + 32] = lim[g][None, :]
                LT[:, 2, q, cs][32 * b:32 * b + 32] = ldt[g]
                if valid:
                    ps = slice(32 * b + 16 * g2p, 32 * b + 16 * g2p + 16)
                    BT[ps, 0, q, cs] = bre[g].T
                    BT[ps, 1, q, cs] = bim[g].T
            if valid:
                W5[:, q, 32 * b:32 * b + 32] = w_in[:, 32 * P:32 * P + 32]
                GV[q, 32 * b:32 * b + 32, :] = gv[32 * P:32 * P + 32, :]
                GG[q, 32 * b:32 * b + 32, :] = gg[32 * P:32 * P + 32, :]
                D5[32 * b:32 * b + 32, q] = d5[32 * P:32 * P + 32]
    LT[96:128, 0] = LT[0:32, 0]; LT[96:128, 1] = LT[0:32, 1]; LT[96:128, 2] = LT[0:32, 2]
    LCc = np.zeros((128, 3, 16), np.float32); CC = np.zeros((128, 2, 16, 32), np.float32)
    for P in range(16):
        for g2 in range(2):
            g = 2 * P + g2
            ps = slice(64 * g2, 64 * g2 + 64)
            LCc[ps, 0, P] = lre[g]; LCc[ps, 1, P] = lim[g]; LCc[ps, 2, P] = ldt[g]
            CC[ps, 0, P, 16 * g2:16 * g2 + 16] = cre[g].T
            CC[ps, 1, P, 16 * g2:16 * g2 + 16] = cim[g].T
    return (LT.reshape(128, 3, 768), BT.reshape(128, 2, 768), LCc, CC.reshape(128, 2, 512), D5,
            W5.reshape(D, 768), GV.reshape(768, D), GG.reshape(768, D))


def make_in_maps(inputs):
    f = lambda a: np.ascontiguousarray(np.asarray(a, dtype=np.float32))
    x = np.asarray(inputs['x'], np.float32)
    p = np.asarray(inputs['p'], np.float32)
    gl = lambda v: np.asarray(v, np.float32).reshape(8, 128).T
    gains = np.stack([gl(inputs['ffn1_norm'][0]), gl(inputs['mix_norm'][0]), gl(inputs['ffn2_norm'][0]),
                      gl(inputs['ple_norm'][0]), gl(inputs['final_norm']), gl(inputs['final_norm'])], axis=1)
    consts = np.zeros((128, 768), np.float32)
    consts[:, 0:512] = 1.0
    consts[:, 0:512:64] = 0.0
    ii = np.arange(128)
    consts[:, 512:640] = ((ii[None, :] >= ii[:, None]) & ((ii[None, :] // 64) == (ii[:, None] // 64))).astype(np.float32)
    consts[:, 640:768] = np.eye(128, dtype=np.float32)
    LT, BT, LCc, CC, D5, W5h, GVh, GGh = _s5_layouts(inputs)
    hglb = np.asarray(inputs['hg_lower_bound'], np.float32).reshape(2, 8, 128).transpose(2, 0, 1)
    shared = {
        'gains': f(gains), 'consts': consts,
        'ffn1_w_gate': f(inputs['ffn1_w_gate'][0]), 'ffn1_w_up': f(inputs['ffn1_w_up'][0]),
        'ffn1_w_down': f(inputs['ffn1_w_down'][0]),
        'ffn2_w_gate': f(inputs['ffn2_w_gate'][0]), 'ffn2_w_up': f(inputs['ffn2_w_up'][0]),
        'ffn2_w_down': f(inputs['ffn2_w_down'][0]),
        'ple_w_gate': f(inputs['ple_w_gate'][0]), 'ple_w_proj': f(inputs['ple_w_proj'][0]),
        'w_in': f(inputs['w_in'][0]),
        's5_LT': f(LT), 's5_BT': f(BT), 's5_LCc': f(LCc), 's5_CC': f(CC), 's5_D5': f(D5), 's5_w5': f(W5h),
        's5_glu_val': f(GVh), 's5_glu_gate': f(GGh),
        'hg_lb': f(hglb), 'hg_gn': f(gl(inputs['hg_out_norm'][0])),
        'hg_w_out': f(inputs['hg_w_out'][0]), 'w_merge_out': f(inputs['w_merge_out'][0]),
    }
    maps = []
    for c in range(NCORES):
        b, hf = c // 2, c % 2
        m = dict(shared)
        m['xT'] = f(x[b, hf * T:(hf + 1) * T, :].T)
        m['pT'] = f(p[0, b, hf * T:(hf + 1) * T, :].T)
        m['flag'] = np.full((128, 1), float(hf), np.float32)
        maps.append(m)
    return maps


def assemble(results):
    out = np.empty((4, 4096, D), np.float32)
    for c in range(NCORES):
        b, hf = c // 2, c % 2
        out[b, hf * T:(hf + 1) * T, :] = results[c]["outT"].T
    return out


_NC_CACHE = {}


def kernel(**inputs):
    if 'nc' not in _NC_CACHE:
        _NC_CACHE['nc'] = build()
    nc = _NC_CACHE['nc']
    res = run_bass_kernel_spmd(nc, make_in_maps(inputs), core_ids=list(range(NCORES)))
    return assemble(res.results)
```
